# Optimizing a Trainium2 kernel written in Bass

```python
import math
import jax
import jax.numpy as jnp
from jax import lax
import numpy as np

D_MODEL = 1024
BATCH = 8
SEQ = 2048
DEPTH = 2
DEC_BATCH = 128
DEC_SEQ = 8
PAST_LEN = 2048
PAGE_SIZE = 128

D_MIX = D_MODEL
HEAD_DIM = 64
D_ATT = (3 * D_MIX) // 8
D_REC = (3 * D_MIX) // 8
D_CONV = D_MIX - D_ATT - D_REC
N_HEADS_ATT = D_ATT // HEAD_DIM
N_HEADS_REC = D_REC // HEAD_DIM
DILATED_BRANCHES = ((128, 1), (512, 4), (2048, 16))
WIN_MAX = max(w for w, _ in DILATED_BRANCHES)
ROPE_THETA = 10000.0
CONV_WIDTH = 3
D_FF = -(-8 * D_MODEL // (3 * 256)) * 256
D_IN_PROJ = 3 * D_ATT + 4 * D_REC + 3 * D_CONV
Q_BLOCK = 128
REC_CHUNK = 64
NORM_EPS = 1e-6

kernel_name = 'hymba_dilated_hgrn2_shortconv_step'


def _rms(x):
    xf = x.astype(jnp.float32)
    return xf * lax.rsqrt(jnp.mean(xf * xf, axis=-1, keepdims=True) + NORM_EPS)


def _modulate(x, shift, scale):
    h = _rms(x) * (1.0 + scale[:, None, :].astype(jnp.float32)) + shift[:, None, :].astype(jnp.float32)
    return h.astype(x.dtype)


def _rope(x, pos):
    half = HEAD_DIM // 2
    freqs = ROPE_THETA ** (-jnp.arange(half, dtype=jnp.float32) / half)
    ang = pos.astype(jnp.float32)[:, None] * freqs[None, :]
    cos = jnp.cos(ang)[None, :, None, :]
    sin = jnp.sin(ang)[None, :, None, :]
    xf = x.astype(jnp.float32)
    x1, x2 = xf[..., :half], xf[..., half:]
    return jnp.concatenate([x1 * cos - x2 * sin, x2 * cos + x1 * sin], axis=-1).astype(x.dtype)


def _dilated_attention(q, k_all, v_all, q_idx):
    B, T, H, hd = q.shape
    qb = math.gcd(T, Q_BLOCK)
    nb = T // qb
    q_blocks = q.reshape(B, nb, qb, H, hd).transpose(1, 0, 2, 3, 4)
    idx_blocks = q_idx.reshape(nb, qb)
    scale = hd ** -0.5

    def block(args):
        qblk, qi = args
        outs, lses = [], []
        for win, dil in DILATED_BRANCHES:
            j = jnp.arange(win // dil + 1, dtype=jnp.int32)
            idx = qi[:, None] - dil * j[None, :]
            valid = idx >= 0
            idx = jnp.maximum(idx, 0)
            kg = k_all[:, idx]
            vg = v_all[:, idx]
            s = jnp.einsum('bqhd,bqjhd->bhqj', qblk, kg).astype(jnp.float32) * scale
            s = jnp.where(valid[None, None], s, -jnp.inf)
            m = jnp.max(s, axis=-1, keepdims=True)
            p = jnp.exp(s - m)
            den = jnp.sum(p, axis=-1, keepdims=True)
            o = jnp.einsum('bhqj,bqjhd->bqhd', p, vg.astype(jnp.float32)) / den.transpose(0, 2, 1, 3)
            outs.append(o)
            lses.append((m + jnp.log(den))[..., 0])
        alpha = jax.nn.softmax(jnp.stack(lses), axis=0)
        o = jnp.einsum('nbhq,nbqhd->bqhd', alpha, jnp.stack(outs))
        return o.astype(q.dtype)

    out = lax.map(block, (q_blocks, idx_blocks))
    return out.transpose(1, 0, 2, 3, 4).reshape(B, T, H, hd)


def _hgrn2(q, log_f, k, v, s0):
    B, T, H, K = q.shape
    C = math.gcd(T, REC_CHUNK)
    nc = T // C

    def chunks(a):
        return a.astype(jnp.float32).reshape(B, nc, C, H, a.shape[-1]).transpose(1, 0, 3, 2, 4)

    tri = jnp.tril(jnp.ones((C, C), dtype=bool))

    def step(S, inp):
        qc, lfc, kc, vc = inp
        b = jnp.cumsum(lfc, axis=2)
        diff = b[:, :, :, None, :] - b[:, :, None, :, :]
        decay = jnp.where(tri[:, :, None], jnp.exp(jnp.minimum(diff, 0.0)), 0.0)
        a = jnp.einsum('bhtk,bhsk,bhtsk->bhts', qc, kc, decay)
        o = jnp.einsum('bhts,bhsv->bhtv', a, vc) + jnp.einsum('bhtk,bhkv->bhtv', qc * jnp.exp(b), S)
        b_last = b[:, :, -1:, :]
        S = jnp.exp(b_last[:, :, 0, :])[..., None] * S + jnp.einsum('bhsk,bhsv->bhkv', kc * jnp.exp(b_last - b), vc)
        return S, o

    S_fin, o = lax.scan(step, s0.astype(jnp.float32), (chunks(q), chunks(log_f), chunks(k), chunks(v)))
    o = o.transpose(1, 0, 3, 2, 4).reshape(B, T, H, v.shape[-1])
    return o, S_fin


def _short_conv(u, buf, w, b):
    T = u.shape[1]
    up = jnp.concatenate([buf.astype(u.dtype), u], axis=1)
    y = up[:, 0:T] * w[0]
    for i in range(1, CONV_WIDTH):
        y = y + up[:, i:i + T] * w[i]
    return y + b, up[:, -(CONV_WIDTH - 1):]


def _layer(x, c, pos, k_buf, v_buf, s0, conv0, lb, w_ada, b_ada, w_in, conv_w, conv_b,
           norm_w, w_out, w_ffn_in, w_ffn_out):
    B, T, _ = x.shape
    mod = jax.nn.silu(c) @ w_ada + b_ada
    sh1, sc1, g1, sh2, sc2, g2 = jnp.split(mod, 6, axis=-1)
    h = _modulate(x, sh1, sc1)
    proj = h @ w_in
    cuts = np.cumsum([D_ATT] * 3 + [D_REC] * 4 + [D_CONV] * 2).tolist()
    qa, ka, va, qr, fr, ir, gr, bc, cc, xc = jnp.split(proj, cuts, axis=-1)

    def heads(a):
        return a.reshape(B, T, -1, HEAD_DIM)

    qa = _rope(heads(qa), pos)
    ka = _rope(heads(ka), pos)
    va = heads(va)
    if k_buf is None:
        k_all, v_all, off = ka, va, 0
    else:
        k_all = jnp.concatenate([k_buf.astype(ka.dtype), ka], axis=1)
        v_all = jnp.concatenate([v_buf.astype(va.dtype), va], axis=1)
        off = k_buf.shape[1]
    q_idx = off + jnp.arange(T, dtype=jnp.int32)
    oa = _dilated_attention(qa, k_all, v_all, q_idx).reshape(B, T, D_ATT)
    keep = min(WIN_MAX, T)
    k_new, v_new = ka[:, T - keep:], va[:, T - keep:]

    z = fr.astype(jnp.float32)
    log_f = jnp.logaddexp(jnp.log(lb), jnp.log1p(-lb) + jax.nn.log_sigmoid(z))
    k_r = (1.0 - lb) * jax.nn.sigmoid(-z)
    o_r, s_new = _hgrn2(heads(qr), heads(log_f), heads(k_r), heads(ir), s0)
    o_r = (_rms(o_r).reshape(B, T, D_REC) * norm_w.astype(jnp.float32)
           * jax.nn.silu(gr.astype(jnp.float32))).astype(x.dtype)

    y_c, conv_new = _short_conv(cc * xc, conv0, conv_w, conv_b)
    oc = bc * y_c

    mix = jnp.concatenate([oa, o_r, oc.astype(x.dtype)], axis=-1) @ w_out
    x = x + g1[:, None, :] * mix

    h2 = _modulate(x, sh2, sc2)
    gate, up = jnp.split(h2 @ w_ffn_in, 2, axis=-1)
    x = x + g2[:, None, :] * ((jax.nn.silu(gate) * up) @ w_ffn_out)
    return x, k_new, v_new, s_new.astype(x.dtype), conv_new


def _trunk(x, c, pos, cache_k, cache_v, state_hgrn, state_conv, lbs, params, final_norm_w):
    w_ada, b_ada, w_in, conv_w, conv_b, hgrn_norm_w, w_out, w_ffn_in, w_ffn_out = params
    B = x.shape[0]
    ks, vs, hs, cs = [], [], [], []
    for l in range(DEPTH):
        if cache_k is None:
            k_buf, v_buf = None, None
            s0 = jnp.zeros((B, N_HEADS_REC, HEAD_DIM, HEAD_DIM), jnp.float32)
            conv0 = jnp.zeros((B, CONV_WIDTH - 1, D_CONV), x.dtype)
        else:
            k_buf, v_buf, s0, conv0 = cache_k[l], cache_v[l], state_hgrn[l], state_conv[l]
        x, k_new, v_new, s_new, conv_new = _layer(
            x, c, pos, k_buf, v_buf, s0, conv0, lbs[l], w_ada[l], b_ada[l], w_in[l], conv_w[l],
            conv_b[l], hgrn_norm_w[l], w_out[l], w_ffn_in[l], w_ffn_out[l])
        ks.append(k_new)
        vs.append(v_new)
        hs.append(s_new)
        cs.append(conv_new)
    y = (_rms(x) * final_norm_w.astype(jnp.float32)).astype(x.dtype)
    return y, jnp.stack(ks), jnp.stack(vs), jnp.stack(hs), jnp.stack(cs)


def setup_inputs(seed: int = 0) -> dict:
    key = jax.random.key(seed)
    ks = jax.random.split(key, 20)
    wbuf = min(WIN_MAX, PAST_LEN)
    f32 = jnp.float32
    nrm = lambda k, s: jax.random.normal(k, s, f32)
    return {
        'x_prompt': nrm(ks[0], (BATCH, SEQ, D_MODEL)),
        'x_sample': nrm(ks[1], (DEC_BATCH, DEC_SEQ, D_MODEL)),
        'cache_k': nrm(ks[2], (DEPTH, DEC_BATCH, wbuf, N_HEADS_ATT, HEAD_DIM)),
        'cache_v': nrm(ks[3], (DEPTH, DEC_BATCH, wbuf, N_HEADS_ATT, HEAD_DIM)),
        'state_hgrn': 0.5 * nrm(ks[4], (DEPTH, DEC_BATCH, N_HEADS_REC, HEAD_DIM, HEAD_DIM)),
        'state_conv': nrm(ks[5], (DEPTH, DEC_BATCH, CONV_WIDTH - 1, D_CONV)),
        'c_prompt': nrm(ks[6], (BATCH, D_MODEL)),
        'c_sample': nrm(ks[7], (DEC_BATCH, D_MODEL)),
        'w_ada': 0.5 * D_MODEL ** -0.5 * nrm(ks[8], (DEPTH, D_MODEL, 6 * D_MODEL)),
        'b_ada': 0.01 * nrm(ks[9], (DEPTH, 6 * D_MODEL)),
        'w_in': D_MODEL ** -0.5 * nrm(ks[10], (DEPTH, D_MODEL, D_IN_PROJ)),
        'conv_w': CONV_WIDTH ** -0.5 * nrm(ks[11], (DEPTH, CONV_WIDTH, D_CONV)),
        'conv_b': 0.01 * nrm(ks[12], (DEPTH, D_CONV)),
        'hgrn_lb_logits': 0.5 * nrm(ks[13], (DEPTH, D_REC)),
        'hgrn_norm_w': 1.0 + 0.01 * nrm(ks[14], (DEPTH, D_REC)),
        'w_out': D_MIX ** -0.5 * nrm(ks[15], (DEPTH, D_MIX, D_MODEL)),
        'w_ffn_in': D_MODEL ** -0.5 * nrm(ks[16], (DEPTH, D_MODEL, 2 * D_FF)),
        'w_ffn_out': D_FF ** -0.5 * nrm(ks[17], (DEPTH, D_FF, D_MODEL)),
        'final_norm_w': 1.0 + 0.01 * nrm(ks[18], (D_MODEL,)),
    }


def reference(x_prompt, x_sample, cache_k, cache_v, state_hgrn, state_conv, c_prompt, c_sample,
              w_ada, b_ada, w_in, conv_w, conv_b, hgrn_lb_logits, hgrn_norm_w, w_out,
              w_ffn_in, w_ffn_out, final_norm_w):
    cum = jnp.cumsum(jax.nn.softmax(hgrn_lb_logits.astype(jnp.float32), axis=0), axis=0)
    lbs = cum - cum[0:1]
    params = (w_ada, b_ada, w_in, conv_w, conv_b, hgrn_norm_w, w_out, w_ffn_in, w_ffn_out)
    pos_p = jnp.arange(x_prompt.shape[1], dtype=jnp.int32)
    pos_s = PAST_LEN + jnp.arange(x_sample.shape[1], dtype=jnp.int32)
    y_prompt, k_p, v_p, h_p, cv_p = _trunk(x_prompt, c_prompt, pos_p, None, None, None, None,
                                           lbs, params, final_norm_w)
    y_sample, k_s, v_s, h_s, cv_s = _trunk(x_sample, c_sample, pos_s, cache_k, cache_v, state_hgrn,
                                           state_conv, lbs, params, final_norm_w)
    return (y_prompt, y_sample, k_p, v_p, h_p, cv_p, k_s, v_s, h_s, cv_s)
```

```python
import numpy as np
from contextlib import ExitStack
import concourse.bass as bass
import concourse.mybir as mybir
from concourse.bass_utils import run_bass_kernel_spmd

F32 = mybir.dt.float32
BF16 = mybir.dt.bfloat16
AF = mybir.ActivationFunctionType
ALU = mybir.AluOpType
AX = mybir.AxisListType

ENGS = ("pe", "act", "dve", "pool", "sp")


class Buf:
    __slots__ = ("name", "w", "r", "dsem", "dcount", "nobar")

    def __init__(self, name="", nobar=False):
        self.name = name
        self.w = None
        self.r = []
        self.dsem = None
        self.dcount = 0
        self.nobar = nobar


class Op:
    __slots__ = ("eng", "idx", "fn", "waits", "dwaits", "sig", "clock", "is_dma", "dtok", "trig")

    def __init__(self, eng, idx, fn):
        self.eng = eng
        self.idx = idx
        self.fn = fn
        self.waits = {}
        self.dwaits = []
        self.sig = False
        self.clock = None
        self.is_dma = False
        self.dtok = None
        self.trig = False


class Prog:
    def __init__(self, nc):
        self.nc = nc
        self.q = {e: [] for e in ENGS}
        self.known = {e: {f: -1 for f in ENGS} for e in ENGS}
        self.dknown = {e: {} for e in ENGS}
        self._semctx = []
        self.dbufs = []
        self.dsems = []

    def new_sem(self, name):
        cm = self.nc.semaphore(name)
        s = cm.__enter__()
        self._semctx.append(cm)
        return s

    def close(self):
        for cm in reversed(self._semctx):
            cm.__exit__(None, None, None)
        self._semctx = []

    def _need_dtok(self, op, tok):
        sem, val = tok
        kd = self.dknown[op.eng]
        if kd.get(id(sem), -1) < val:
            kd[id(sem)] = val
            op.dwaits.append((sem, val))

    def _deps(self, op, reads, writes):
        eng = op.eng
        need = []
        for b in reads:
            if b.w is not None:
                need.append((b.w, "raw"))
        for b in writes:
            if b.w is not None:
                need.append((b.w, "waw"))
            for r in b.r:
                need.append((r, "war"))
        dmax = {}
        for d, kind in need:
            if d.is_dma:
                sem, val = d.dtok
                if id(sem) not in dmax or dmax[id(sem)][1] < val:
                    dmax[id(sem)] = (sem, val)
        for tok in dmax.values():
            self._need_dtok(op, tok)
        for d, kind in need:
            if d.is_dma:
                continue
            if d.eng == eng:
                if eng in ("pe", "sp"):
                    continue
            if self.known[eng][d.eng] >= d.idx:
                continue
            cur = op.waits.get(d.eng)
            if cur is None or cur.idx < d.idx:
                op.waits[d.eng] = d
        k = self.known[eng]
        for f, d in op.waits.items():
            d.sig = True
            if d.clock is not None:
                for g, v in d.clock.items():
                    if k[g] < v:
                        k[g] = v
            if k[f] < d.idx:
                k[f] = d.idx

    def op(self, eng, fn, reads=(), writes=()):
        reads = [b for b in reads if b is not None]
        writes = [b for b in writes if b is not None]
        o = Op(eng, len(self.q[eng]), fn)
        self._deps(o, reads, writes)
        o.clock = dict(self.known[eng])
        o.clock[eng] = o.idx
        for b in reads:
            b.r.append(o)
        for b in writes:
            b.w = o
            b.r = []
        self.q[eng].append(o)
        return o

    def dma(self, eng, out_ap, in_ap, reads=(), writes=(), sbuf=None, **kw):
        reads = list(reads)
        writes = list(writes)
        cls = "sw" if eng == "pool" else "hw"
        if sbuf.dsem is None:
            sbuf.dsem = {}
            sbuf.dcount = {}
        if cls not in sbuf.dsem:
            sbuf.dsem[cls] = self.new_sem("d%d" % len(self.dsems))
            sbuf.dcount[cls] = 0
            self.dsems.append((sbuf, cls))
        sbuf.dcount[cls] += 16
        tok = (sbuf.dsem[cls], sbuf.dcount[cls])

        def fn(e, out_ap=out_ap, in_ap=in_ap, tok=tok, kw=kw):
            return e.dma_start(out=out_ap, in_=in_ap, **kw).then_inc(tok[0], 16)

        o = Op(eng, len(self.q[eng]), fn)
        o.trig = True
        self._deps(o, reads, writes)
        o.clock = dict(self.known[eng])
        self.q[eng].append(o)
        d = Op("dma", -1, None)
        d.is_dma = True
        d.dtok = tok
        for b in reads:
            b.r.append(d)
        for b in writes:
            b.w = d
            b.r = []
        return d

    def _all_dtoks(self):
        return [(b.dsem[c], b.dcount[c]) for b, c in self.dsems]

    def wait_dmas(self, eng="sp"):
        def fn(e):
            return e.nop()
        o = Op(eng, len(self.q[eng]), fn)
        for tok in self._all_dtoks():
            self._need_dtok(o, tok)
        o.clock = dict(self.known[eng])
        o.clock[eng] = o.idx
        self.q[eng].append(o)

    def barrier(self):
        marks = {}
        for e in ENGS:
            m = None
            for o in reversed(self.q[e]):
                if not o.trig:
                    m = o
                    break
            marks[e] = m
        for e in ENGS:
            def fn(en):
                return en.nop()
            o = Op(e, len(self.q[e]), fn)
            for f in ENGS:
                m = marks[f]
                if f != e and m is not None and self.known[e][f] < m.idx:
                    o.waits[f] = m
                    m.sig = True
                    self.known[e][f] = m.idx
            for b, c in self.dsems:
                if not b.nobar:
                    self._need_dtok(o, (b.dsem[c], b.dcount[c]))
            o.clock = dict(self.known[e])
            o.clock[e] = o.idx
            self.q[e].append(o)

    def emit(self):
        nc = self.nc
        esem = {e: self.new_sem("e_" + e) for e in ENGS}
        signum = {}
        for e in ENGS:
            c = 0
            for o in self.q[e]:
                if o.sig:
                    assert not o.trig
                    c += 1
                    signum[o] = c
        self.stats = {e: (len(self.q[e]), sum(1 for o in self.q[e] if o.sig),
                          sum(len(o.waits) + len(o.dwaits) for o in self.q[e])) for e in ENGS}

        def run(e, engobj):
            for o in self.q[e]:
                for f, d in o.waits.items():
                    engobj.wait_ge(esem[f], signum[d])
                for sem, val in o.dwaits:
                    engobj.wait_ge(sem, val)
                ins = o.fn(engobj)
                if o.sig:
                    ins.then_inc(esem[e], 1)

        with nc.Block() as block:
            @block.tensor
            def _(en):
                run("pe", en)

            @block.scalar
            def _(en):
                run("act", en)

            @block.vector
            def _(en):
                run("dve", en)

            @block.gpsimd
            def _(en):
                run("pool", en)

            @block.sync
            def _(en):
                run("sp", en)


D = 1024
SEQ = 2048
NTOK = SEQ + 128
DEPTH = 2
DFF = 2816
NF = 22
EPS = 1e-6
CP = 32
GP = 4
CS = 8
GS = 16
NTB = 256
TPB = NTB // 128
NPB = SEQ // NTB
C_QA, C_KA, C_VA, C_QR, C_FR, C_IR, C_GR, C_BC, C_CC, C_XC = 0, 384, 768, 1152, 1536, 1920, 2304, 2688, 2944, 3200


def _mult(delta):
    m = 0
    if 0 <= delta <= 128:
        m += 1
    if delta % 4 == 0 and 0 <= delta <= 512:
        m += 1
    if delta % 16 == 0 and 0 <= delta <= 2048:
        m += 1
    return m


def host_consts():
    c = {}
    c["ident"] = np.eye(128, dtype=np.float32)
    half = 32
    freqs = (10000.0 ** (-np.arange(half, dtype=np.float32) / half)).astype(np.float32)
    pos_p = np.arange(SEQ, dtype=np.float32)
    ang = pos_p[:, None] * freqs[None, :]
    c["cos_p"] = np.cos(ang).astype(np.float32)
    c["sin_p"] = np.sin(ang).astype(np.float32)
    pos_s = (2048 + np.arange(8, dtype=np.float32))
    ang_s = np.tile(pos_s[:, None] * freqs[None, :], (16, 1))
    c["cos_s"] = np.cos(ang_s).astype(np.float32)
    c["sin_s"] = np.sin(ang_s).astype(np.float32)
    kl = np.arange(128)[:, None]
    x = np.arange(19 * 128)[None, :]
    dl = x - 384 - kl
    mv = np.vectorize(_mult)
    c["mstrip"] = mv(dl).astype(np.float32)
    g = np.arange(128)[:, None, None]
    r = np.arange(16)[None, :, None]
    t = np.arange(8)[None, None, :]
    c["mc"] = mv(2048 + t - 16 * g - r).astype(np.float32).reshape(128, 128)
    bp = np.arange(128)[:, None] // 8
    tp = np.arange(128)[:, None] % 8
    bq = np.arange(128)[None, :] // 8
    tq = np.arange(128)[None, :] % 8
    c["mnew"] = (mv(tq - tp) * (bp == bq)).astype(np.float32)
    s = np.arange(128)[:, None]
    tt = np.arange(128)[None, :]
    for nm, C in (("P", CP), ("S", CS)):
        same = (s // C) == (tt // C)
        c["tri" + nm] = (same & (s <= tt)).astype(np.float32)
        c["u" + nm] = (same & (s > tt)).astype(np.float32)
    c["rowmP"] = (np.arange(128)[:, None] // CP == np.arange(GP)[None, :]).astype(np.float32)
    c["rowmS"] = (np.arange(128)[:, None] // CS == np.arange(GS)[None, :]).astype(np.float32)
    return c


CONST_SHAPES = {"ident": [128, 128], "cos_p": [SEQ, 32], "sin_p": [SEQ, 32], "cos_s": [128, 32], "sin_s": [128, 32],
                "mstrip": [128, 19 * 128], "mc": [128, 128], "mnew": [128, 128], "triP": [128, 128], "uP": [128, 128],
                "triS": [128, 128], "uS": [128, 128], "rowmP": [128, GP], "rowmS": [128, GS]}

IN_SHAPES = {
    "xp": [SEQ, D], "xs": [128, D], "call": [17, D],
    "ck": [DEPTH, 16, 2048, 384], "cv": [DEPTH, 16, 2048, 384],
    "sh": [DEPTH, 16, 6, 64, 64], "sc": [DEPTH, 16, 2, 256],
    "w_ada": [DEPTH, D, 6 * D], "b_ada": [DEPTH, 48, 128], "w_in": [DEPTH, D, 3456],
    "conv_w": [DEPTH, 3, 256], "conv_b": [DEPTH, 256], "lbl": [DEPTH, 384], "nw": [DEPTH, 384],
    "w_out": [DEPTH, D, D], "w_ffn_in": [DEPTH, D, 2 * DFF], "w_ffn_out": [DEPTH, DFF, D], "fnw": [8, 128],
}
OUT_SHAPES = {
    "yp": [SEQ, D], "ys": [128, D], "kp": [DEPTH, SEQ, 384], "vp": [DEPTH, SEQ, 384],
    "hp": [DEPTH, 6, 64, 64], "cvp": [DEPTH, 2, 256], "ks": [DEPTH, 128, 384], "vs": [DEPTH, 128, 384],
    "hs": [DEPTH, 16, 6, 64, 64], "cvs": [DEPTH, 16, 2, 256],
}


def sap(t, off, dims):
    return bass.AP(t, off, dims)


def build_program(stage=99):
    nc = bass.Bass("TRN2", target_bir_lowering=False)
    I = {n: nc.dram_tensor(n, s, F32, kind="ExternalInput").ap() for n, s in IN_SHAPES.items()}
    CI = {n: nc.dram_tensor("c_" + n, s, F32, kind="ExternalInput").ap() for n, s in CONST_SHAPES.items()}
    O = {n: nc.dram_tensor(n, s, F32, kind="ExternalOutput").ap() for n, s in OUT_SHAPES.items()}
    P = Prog(nc)
    es = ExitStack()

    def sb(name, shape, dt=F32):
        return es.enter_context(nc.sbuf_tensor(name, shape, dt))

    es0 = ExitStack()

    def sb0(name, shape, dt=F32):
        return es0.enter_context(nc.sbuf_tensor(name, shape, dt))

    xTb = sb("xTb", [128, 8, NTB])
    modT = sb("modT", [128, DEPTH, 48, 17])
    ident = sb("ident", [128, 128])
    identb = sb("identb", [128, 128], BF16)
    onesb = sb("onesb", [128, 128], BF16)
    cosp = sb("cosp", [128, 16, 32])
    sinp = sb("sinp", [128, 16, 32])
    coss = sb("coss", [128, 32])
    sins = sb("sins", [128, 32])
    mstrip = sb("mstrip", [128, 19 * 128], BF16)
    mc = sb("mc", [128, 128], BF16)
    mnew = sb("mnew", [128, 128], BF16)
    triP = sb("triP", [128, 128])
    uP = sb("uP", [128, 128])
    triS = sb("triS", [128, 128])
    uS = sb("uS", [128, 128])
    triPb = sb("triPb", [128, 128], BF16)
    triSb = sb("triSb", [128, 128], BF16)
    rowmP = sb("rowmP", [128, GP], BF16)
    rowmS = sb("rowmS", [128, GS], BF16)
    oml_tm = sb("oml_tm", [128, DEPTH, 384])
    oml_fm = sb("oml_fm", [128, DEPTH, 3])
    nw_tm = sb("nw_tm", [128, DEPTH, 384])
    cw_fm = sb("cw_fm", [128, DEPTH, 2, 3])
    cb_fm = sb("cb_fm", [128, DEPTH, 2])
    fnw_fm = sb("fnw_fm", [128, 8])
    epsb = sb("epsb", [128, 1])
    ps = es.enter_context(nc.psum_tensor("ps", [128, 8, 512], F32))
    PB = [Buf("psb%d" % i) for i in range(8)]

    B_xT = [Buf("xT%d" % k) for k in range(8)]
    B_const = Buf("const")
    B_mod = [Buf("mod0"), Buf("mod1")]

    def psb(i):
        return ps[:, i, :]

    def psbf(i):
        return ps[:, i, :].bitcast(BF16)

    def mm(out, lhsT, rhs, start, stop, reads, bank):
        P.op("pe", lambda e: e.matmul(out, lhsT, rhs, start=start, stop=stop, skip_group_check=True),
             reads=reads, writes=[bank])

    def tr(out, in_, idn, reads, bank):
        P.op("pe", lambda e: e.transpose(out, in_, idn), reads=reads, writes=[bank])

    def act(out, in_, func, reads, writes, scale=1.0, bias=None, accum=None):
        def fn(e):
            kw = {}
            if bias is not None:
                kw["bias"] = bias
            if accum is not None:
                kw["accum_out"] = accum
            return e.activation(out, in_, func, scale=scale, **kw)
        P.op("act", fn, reads=reads, writes=writes)

    def vop(fn, reads, writes, eng="dve"):
        P.op(eng, fn, reads=reads, writes=writes)

    NSLOT = 3
    SLOTN = 3072
    wslots = [sb("wslot%d" % i, [128, SLOTN], BF16) for i in range(NSLOT)]
    wbufs = [Buf("wslot%d" % i) for i in range(NSLOT)]
    wstate = {"n": 0}
    scr = {}
    B_scr = [{k: Buf("scr%d%s" % (l, k), nobar=True) for k in ("tm", "fm", "wo", "fi", "fo")} for l in range(DEPTH)]

    def wslot_next():
        i = wstate["n"] % NSLOT
        wstate["n"] += 1
        return wslots[i], wbufs[i]

    def wreq(key, src_ap, shape, pieces=None):
        t, b = wslot_next()
        n = int(np.prod(shape[1:]))
        assert n <= SLOTN
        view = t[:, 0:n].rearrange("p (a b) -> p a b", a=shape[1])
        P.dma("pool", view, src_ap, writes=[b], sbuf=b)
        return view, b

    def wspec(kind, l, idx):
        if kind == "tm":
            c0 = [C_QA, C_KA, C_VA, C_FR, C_IR, C_GR][idx]
            return 3072, [(w_rows(I["w_in"][l], c0, 384), 0, [8, 384])]
        if kind == "fm":
            c0 = [C_QR, C_FR, C_BC, C_BC + 384][idx]
            return 3072, [(w_rows(I["w_in"][l], c0, 384), 0, [8, 384])]
        if kind == "wo":
            return 2048, [(w_rows(I["w_out"][l], idx * 256, 256), 0, [8, 256])]
        if kind == "fi":
            return 2048, [(w_rows(I["w_ffn_in"][l], idx * 128, 128), 0, [8, 128]),
                          (w_rows(I["w_ffn_in"][l], DFF + idx * 128, 128), 1024, [8, 128])]
        if kind == "fo":
            return NF * 128, [(I["w_ffn_out"][l][:, idx * 128:(idx + 1) * 128].rearrange("(f p) c -> p f c", p=128), 0, [NF, 128])]
        raise KeyError(kind)

    WKINDS = [("tm", 6), ("fm", 4), ("wo", 4), ("fi", NF), ("fo", 8)]

    def prepass(first_reads=()):
        fr = list(first_reads)
        for l in range(DEPTH):
            for kind, cnt in WKINDS:
                for idx in range(cnt):
                    n, pieces = wspec(kind, l, idx)
                    d = nc.dram_tensor("scr_%s_%d_%d" % (kind, l, idx), [128, n], BF16).ap()
                    scr[(kind, l, idx)] = (d, n)
                    for src_ap, off, shp in pieces:
                        P.dma("pool", d[:, off:off + shp[0] * shp[1]].rearrange("p (a b) -> p a b", a=shp[0]), src_ap,
                              reads=fr, writes=[B_scr[l][kind]], sbuf=B_scr[l][kind])
                        fr = []

    def wget(kind, l, idx):
        d, n = scr[(kind, l, idx)]
        t, b = wslot_next()
        P.dma("sp", t[:, 0:n], d[:, 0:n], reads=[B_scr[l][kind]], writes=[b], sbuf=b)
        return t, b

    def w_rows(wap, c0, n):
        return wap[:, c0:c0 + n].rearrange("(k p) n -> p k n", p=128)

    P.dma("sp", ident[:, :], CI["ident"], writes=[B_const], sbuf=B_const)
    P.dma("sp", cosp[:, :, :], CI["cos_p"].rearrange("(j p) f -> p j f", p=128), writes=[B_const], sbuf=B_const)
    P.dma("sp", sinp[:, :, :], CI["sin_p"].rearrange("(j p) f -> p j f", p=128), writes=[B_const], sbuf=B_const)
    P.dma("sp", coss[:, :], CI["cos_s"], writes=[B_const], sbuf=B_const)
    P.dma("sp", sins[:, :], CI["sin_s"], writes=[B_const], sbuf=B_const)
    for nm, t in (("triP", triP), ("uP", uP), ("triS", triS), ("uS", uS)):
        P.dma("sp", t[:, :], CI[nm], writes=[B_const], sbuf=B_const)
    B_c2 = Buf("const2")
    P.dma("pool", mstrip[:, :], CI["mstrip"], writes=[B_c2], sbuf=B_c2)
    P.dma("pool", mc[:, :], CI["mc"], writes=[B_c2], sbuf=B_c2)
    P.dma("pool", mnew[:, :], CI["mnew"], writes=[B_c2], sbuf=B_c2)
    P.dma("pool", triPb[:, :], CI["triP"], writes=[B_c2], sbuf=B_c2)
    P.dma("pool", triSb[:, :], CI["triS"], writes=[B_c2], sbuf=B_c2)
    P.dma("pool", rowmP[:, :], CI["rowmP"], writes=[B_c2], sbuf=B_c2)
    P.dma("pool", rowmS[:, :], CI["rowmS"], writes=[B_c2], sbuf=B_c2)
    P.dma("pool", identb[:, :], CI["ident"], writes=[B_c2], sbuf=B_c2)
    B_c3 = Buf("const3")
    for l in range(DEPTH):
        P.dma("sp", nw_tm[:, l, :], I["nw"][l:l + 1, :].broadcast_to([128, 384]), writes=[B_c3], sbuf=B_c3)
        P.dma("sp", oml_fm[:, l, :], I["lbl"][l].rearrange("(m p) -> p m", p=128), writes=[B_c3], sbuf=B_c3,
              allow_slow_non_contiguous=True)
        for m in range(2):
            P.dma("sp", cw_fm[:, l, m, :], I["conv_w"][l][:, m * 128:(m + 1) * 128].rearrange("t p -> p t"), writes=[B_c3], sbuf=B_c3,
                  allow_slow_non_contiguous=True)
        P.dma("sp", cb_fm[:, l, :], I["conv_b"][l].rearrange("(m p) -> p m", p=128), writes=[B_c3], sbuf=B_c3,
              allow_slow_non_contiguous=True)
    P.dma("sp", fnw_fm[:, :], I["fnw"].rearrange("k p -> p k"), writes=[B_c3], sbuf=B_c3, allow_slow_non_contiguous=True)
    lbb = sb0("lbb", [128, 2, 384])
    for l in range(DEPTH):
        P.dma("sp", lbb[:, l, :], I["lbl"][l:l + 1, :].broadcast_to([128, 384]), writes=[B_c3], sbuf=B_c3)
    vop(lambda e: e.memset(onesb[:, :], 1.0), [], [B_const], eng="pool")
    vop(lambda e: e.memset(epsb[:, :], EPS), [], [B_const], eng="pool")
    vop(lambda e: e.memset(oml_tm[:, 0, :], 1.0), [], [B_c3], eng="pool")
    lbt = sb0("lbt", [128, 384])
    B_lbt = Buf("lbt")
    vop(lambda e: e.tensor_sub(lbt[:, :], lbb[:, 0, :], lbb[:, 1, :]), [B_c3], [B_lbt])
    act(lbt[:, :], lbt[:, :], AF.Exp, [B_lbt], [B_lbt])
    vop(lambda e: e.tensor_scalar_add(lbb[:, 0, :], lbt[:, :], 1.0), [B_lbt], [B_c3])
    vop(lambda e: e.reciprocal(lbb[:, 0, :], lbb[:, 0, :]), [B_c3], [B_c3])
    vop(lambda e: e.tensor_mul(oml_tm[:, 1, :], lbt[:, :], lbb[:, 0, :]), [B_c3, B_lbt], [B_c3])
    lbf = sb0("lbf", [128, 4, 3])
    vop(lambda e: e.tensor_sub(lbf[:, 0, :], oml_fm[:, 0, :], oml_fm[:, 1, :]), [B_c3], [B_lbt])
    act(lbf[:, 0, :], lbf[:, 0, :], AF.Exp, [B_lbt], [B_lbt])
    vop(lambda e: e.tensor_scalar_add(lbf[:, 1, :], lbf[:, 0, :], 1.0), [B_lbt], [B_lbt])
    vop(lambda e: e.reciprocal(lbf[:, 1, :], lbf[:, 1, :]), [B_lbt], [B_lbt])
    vop(lambda e: e.tensor_mul(oml_fm[:, 1, :], lbf[:, 0, :], lbf[:, 1, :]), [B_lbt, B_c3], [B_c3])
    vop(lambda e: e.memset(oml_fm[:, 0, :], 1.0), [B_c3], [B_c3])

    cin = sb0("cin", [17, D])
    scT = sb0("scT", [128, 8, 17], BF16)
    adab = [sb0("adab%d" % i, [128, 8, 128], BF16) for i in range(2)]
    B_adab = [Buf("adab0"), Buf("adab1")]
    ctmp = sb0("ctmp", [128, 3, 8 * 17])
    badaT = sb0("badaT", [128, DEPTH, 48])
    bin_ = sb0("bin_", [48, DEPTH, 128])
    B_cin, B_scT, B_ctmp, B_bada = Buf("cin"), Buf("scT"), Buf("ctmp"), Buf("bada")
    P.dma("sp", cin[:, :], I["call"], writes=[B_cin], sbuf=B_cin)
    P.dma("sp", bin_[:, :, :], I["b_ada"].rearrange("l m p -> m l p"), writes=[B_bada], sbuf=B_bada)
    for k in range(8):
        tr(ps[:, 0, k * 17:(k + 1) * 17], cin[:, k * 128:(k + 1) * 128], ident[0:17, 0:17], [B_cin, B_const], PB[0])
    act(ctmp[:, 0, :], ps[:, 0, 0:136], AF.Exp, [PB[0]], [B_ctmp, PB[0]], scale=-1.0)
    vop(lambda e: e.tensor_scalar_add(ctmp[:, 0, :], ctmp[:, 0, :], 1.0), [B_ctmp], [B_ctmp])
    vop(lambda e: e.reciprocal(ctmp[:, 0, :], ctmp[:, 0, :]), [B_ctmp], [B_ctmp])
    vop(lambda e: e.tensor_tensor(scT[:, :, :].rearrange("p k c -> p (k c)"), ps[:, 0, 0:136], ctmp[:, 0, :], ALU.mult),
        [B_ctmp, PB[0]], [B_scT, PB[0]])
    for l in range(DEPTH):
        tr(ps[:, 1, l * 48:(l + 1) * 48], bin_[:, l, :], ident[0:48, 0:48], [B_bada, B_const], PB[1])
    vop(lambda e: e.tensor_copy(badaT[:, :, :].rearrange("p l m -> p (l m)"), ps[:, 1, 0:96]), [PB[1]], [B_bada, PB[1]])
    for l in range(DEPTH):
        for c4 in range(12):
            bk = 2 + (c4 % 2)
            for m4 in range(4):
                c = c4 * 4 + m4
                t_, wb = wslot_next()
                wt32 = t_[:, 0:2048].bitcast(F32).rearrange("p (a b) -> p a b", a=8)
                P.dma("sp", wt32, w_rows(I["w_ada"][l], c * 128, 128), writes=[wb], sbuf=wb)
                ai = c % 2
                vop(lambda e, ai=ai, wt32=wt32: e.tensor_copy(adab[ai][:, :, :], wt32), [wb], [B_adab[ai]])
                for k in range(8):
                    mm(ps[:, bk, m4 * 17:(m4 + 1) * 17], adab[ai][:, k, :], scT[:, k, :],
                       (m4 == 0 and k == 0), (k == 7), [B_adab[ai], B_scT], PB[bk])
            vop(lambda e, l=l, c4=c4, bk=bk: e.tensor_tensor(
                modT[:, l, 4 * c4:4 * c4 + 4, :], ps[:, bk, 0:68].rearrange("p (m c) -> p m c", m=4),
                sap(badaT, l * 48 + 4 * c4, [[DEPTH * 48, 128], [1, 4], [0, 17]]), ALU.add),
                [PB[bk], B_bada], [B_mod[l], PB[bk]])
        vop(lambda e, l=l: e.tensor_scalar_add(modT[:, l, 8:16, :], modT[:, l, 8:16, :], 1.0), [B_mod[l]], [B_mod[l]])
        vop(lambda e, l=l: e.tensor_scalar_add(modT[:, l, 32:40, :], modT[:, l, 32:40, :], 1.0), [B_mod[l]], [B_mod[l]])


    prepass(first_reads=[B_mod[0], B_mod[1]])
    P.barrier()
    es0.close()
    xst = [sb("xst%d" % i, [128, D]) for i in range(2)]
    B_xst = [Buf("xst0"), Buf("xst1")]

    def load_x(blkinfo):
        t0, nt, kind, bi = blkinfo
        s = 0
        for j in range(nt // 128):
            src = I["xp"][t0 + j * 128:t0 + (j + 1) * 128, :] if kind == "p" else I["xs"]
            P.dma("sp", xst[s][:, :], src, writes=[B_xst[s]], sbuf=B_xst[s])
            for half in range(2):
                bk = 4 + half
                for kk in range(4):
                    k = half * 4 + kk
                    tr(ps[:, bk, kk * 128:(kk + 1) * 128], xst[s][:, k * 128:(k + 1) * 128], ident[:, :], [B_xst[s], B_const], PB[bk])
                dst = xTb[:, half * 4:half * 4 + 4, j * 128:(j + 1) * 128]
                src_ps = ps[:, bk, :].rearrange("p (k t) -> p k t", k=4)
                wr = [B_xT[k] for k in range(half * 4, half * 4 + 4)] + [PB[bk]]
                if half == 0:
                    act(dst, src_ps, AF.Copy, [PB[bk]], wr)
                else:
                    vop(lambda e, dst=dst, src_ps=src_ps: e.tensor_copy(dst, src_ps), [PB[bk]], wr)

    hm = sb("hm", [128, 8, NTB], BF16)
    B_hm = [Buf("hm%d" % k) for k in range(8)]
    sq = [sb("sq%d" % i, [128, NTB], BF16) for i in range(2)]
    B_sq = [Buf("sq0"), Buf("sq1")]
    rstd = sb("rstd", [128, NTB])
    B_rstd = Buf("rstd")
    ntmp = [sb("ntmp%d" % i, [128, 256]) for i in range(2)]
    B_ntmp = [Buf("ntmp0"), Buf("ntmp1")]

    BLOCKS = [(i * NTB, NTB, "p", i) for i in range(NPB)] + [(2048, 128, "s", NPB)]

    def mod_bc(l, chunk):
        off = (l * 48 + chunk) * 17 + 1
        return sap(modT, off, [[DEPTH * 48 * 17, 128], [1, 16], [0, 8]])

    def ssq_rstd(nt):
        for k in range(8):
            s = k % 2
            act(sq[s][:, 0:nt], xTb[:, k, 0:nt], AF.Square, [B_xT[k]], [B_sq[s]])
            mm(ps[:, 0, 0:nt], onesb[:, :], sq[s][:, 0:nt], k == 0, k == 7, [B_sq[s], B_const], PB[0])
        act(rstd[:, 0:nt], ps[:, 0, 0:nt], AF.Ln, [PB[0], B_const], [B_rstd, PB[0]], scale=1.0 / D, bias=epsb[:, 0:1])
        act(rstd[:, 0:nt], rstd[:, 0:nt], AF.Exp, [B_rstd], [B_rstd], scale=-0.5)

    def norm_block(l, blkinfo, which):
        t0, nt, kind, bi = blkinfo
        sh0 = 0 if which == 0 else 24
        sc0 = 8 if which == 0 else 32
        ssq_rstd(nt)
        for k in range(8):
            s = k % 2
            vop(lambda e, k=k, s=s: e.tensor_mul(ntmp[s][:, 0:nt], xTb[:, k, 0:nt], rstd[:, 0:nt]),
                [B_xT[k], B_rstd], [B_ntmp[s]])
            if kind == "p":
                act(hm[:, k, 0:nt], ntmp[s][:, 0:nt], AF.Identity, [B_ntmp[s], B_mod[l]], [B_hm[k]],
                    scale=modT[:, l, sc0 + k, 0:1], bias=modT[:, l, sh0 + k, 0:1])
            else:
                v3 = ntmp[s][:, 0:128].rearrange("p (b t) -> p b t", b=16)
                vop(lambda e, v3=v3, k=k: e.tensor_mul(v3, v3, mod_bc(l, sc0 + k)), [B_ntmp[s], B_mod[l]], [B_ntmp[s]])
                vop(lambda e, v3=v3, k=k: e.tensor_add(hm[:, k, 0:128].rearrange("p (b t) -> p b t", b=16), v3, mod_bc(l, sh0 + k)),
                    [B_ntmp[s], B_mod[l]], [B_hm[k]])

    def resid_update(l, blkinfo, m, bank, gchunk0):
        t0, nt, kind, bi = blkinfo
        if kind == "p":
            vop(lambda e: e.scalar_tensor_tensor(xTb[:, m, 0:nt], ps[:, bank, 0:nt], modT[:, l, gchunk0 + m, 0:1],
                                                 xTb[:, m, 0:nt], op0=ALU.mult, op1=ALU.add),
                [PB[bank], B_mod[l], B_xT[m]], [B_xT[m], PB[bank]])
        else:
            s = m % 2
            v3 = ntmp[s][:, 0:128].rearrange("p (b t) -> p b t", b=16)
            vop(lambda e: e.tensor_tensor(v3, ps[:, bank, 0:128].rearrange("p (b t) -> p b t", b=16), mod_bc(l, gchunk0 + m), ALU.mult),
                [PB[bank], B_mod[l]], [B_ntmp[s], PB[bank]])
            vop(lambda e: e.tensor_add(xTb[:, m, 0:128], xTb[:, m, 0:128], ntmp[s][:, 0:128]),
                [B_ntmp[s], B_xT[m]], [B_xT[m]])

    KT = sb("KT", [128, DEPTH, 3, SEQ], BF16)
    Vb = sb("Vb", [128, DEPTH, 16 * 384], BF16)
    B_KT = [[Buf("KT%d_%d" % (l, i)) for i in range(16)] for l in range(DEPTH)]
    B_V = [[Buf("V%d_%d" % (l, i)) for i in range(16)] for l in range(DEPTH)]
    QT = sb("QT", [128, 3, NTB], BF16)
    B_QT = [Buf("QT%d" % j) for j in range(TPB)]
    stg = [sb("stg%d" % i, [128, 384]) for i in range(2)]
    B_stg = [Buf("stg%d" % i) for i in range(2)]
    rp = [sb("rp%d" % i, [128, 384]) for i in range(3)]
    B_rp = [Buf("rp%d" % i) for i in range(3)]
    tb16 = [sb("tb16_%d" % i, [128, 384], BF16) for i in range(2)]
    B_tb16 = [Buf("tb16_0"), Buf("tb16_1")]
    stg_ctr = {"n": 0, "t": 0}

    def rope(src_ps, bank, dst, j, kind, dstbuf):
        if kind == "p":
            cos2 = sap(cosp, j * 32, [[512, 128], [0, 6], [0, 2], [1, 32]])
            sin1 = sap(sinp, j * 32, [[512, 128], [0, 6], [1, 32]])
        else:
            cos2 = sap(coss, 0, [[32, 128], [0, 6], [0, 2], [1, 32]])
            sin1 = sap(sins, 0, [[32, 128], [0, 6], [1, 32]])
        x4 = src_ps.rearrange("p (h two f) -> p h two f", h=6, two=2)
        a4 = rp[0][:, :].rearrange("p (h two f) -> p h two f", h=6, two=2)
        b4 = rp[1][:, :].rearrange("p (h two f) -> p h two f", h=6, two=2)
        vop(lambda e: e.tensor_tensor(a4, x4, cos2, ALU.mult), [PB[bank], B_const], [B_rp[0], PB[bank]])
        vop(lambda e: e.tensor_tensor(b4[:, :, 0, :], x4[:, :, 1, :], sin1, ALU.mult), [PB[bank], B_const], [B_rp[1], PB[bank]])
        vop(lambda e: e.tensor_tensor(b4[:, :, 1, :], x4[:, :, 0, :], sin1, ALU.mult), [PB[bank], B_const], [B_rp[1], PB[bank]])
        d4 = dst.rearrange("p (h two f) -> p h two f", h=6, two=2)
        vop(lambda e: e.tensor_sub(d4[:, :, 0, :], a4[:, :, 0, :], b4[:, :, 0, :]), [B_rp[0], B_rp[1]], [dstbuf])
        vop(lambda e: e.tensor_add(d4[:, :, 1, :], a4[:, :, 1, :], b4[:, :, 1, :]), [B_rp[0], B_rp[1]], [dstbuf])

    def tokmajor_group(l, blkinfo, col0, key, consumer):
        t0, nt, kind, bi = blkinfo
        wt_, wb = wget("tm", l, [C_QA, C_KA, C_VA, C_FR, C_IR, C_GR].index(col0))
        wt = wt_[:, 0:3072].rearrange("p (a b) -> p a b", a=8)
        for j in range(nt // 128):
            bk = 1 + (stg_ctr["t"] % 2)
            stg_ctr["t"] += 1
            for k in range(8):
                mm(ps[:, bk, 0:384], hm[:, k, j * 128:(j + 1) * 128], wt[:, k, :], k == 0, k == 7, [B_hm[k], wb], PB[bk])
            consumer(j, bk)

    def cons_q(l, blkinfo):
        t0, nt, kind, bi = blkinfo

        def f(j, bk):
            rope(ps[:, bk, 0:384], bk, rp[2][:, :], (t0 // 128 + j), kind, B_rp[2])
            i16 = j % 2
            vop(lambda e: e.tensor_copy(tb16[i16][:, :], rp[2][:, :]), [B_rp[2]], [B_tb16[i16]])
            for hp in range(3):
                tr(psbf(3)[:, hp * 128:(hp + 1) * 128], tb16[i16][:, hp * 128:(hp + 1) * 128], identb[:, :], [B_tb16[i16], B_c2], PB[3])
            act(QT[:, :, j * 128:(j + 1) * 128], psbf(3)[:, 0:384].rearrange("p (h t) -> p h t", h=3), AF.Copy,
                [PB[3]], [B_QT[j], PB[3]])
        return f

    def cons_k(l, blkinfo, KTview, B_KTt, out_ap):
        t0, nt, kind, bi = blkinfo

        def f(j, bk):
            si = stg_ctr["n"] % 2
            stg_ctr["n"] += 1
            rope(ps[:, bk, 0:384], bk, stg[si][:, :], (t0 // 128 + j), kind, B_stg[si])
            row0 = (t0 + j * 128) if kind == "p" else 0
            P.dma("pool", out_ap[row0:row0 + 128, :], stg[si][:, :], reads=[B_stg[si]], sbuf=B_stg[si])
            i16 = j % 2
            vop(lambda e: e.tensor_copy(tb16[i16][:, :], stg[si][:, :]), [B_stg[si]], [B_tb16[i16]])
            for hp in range(3):
                tr(psbf(3)[:, hp * 128:(hp + 1) * 128], tb16[i16][:, hp * 128:(hp + 1) * 128], identb[:, :], [B_tb16[i16], B_c2], PB[3])
            tj = (t0 // 128 + j) if kind == "p" else 0
            act(KTview(tj), psbf(3)[:, 0:384].rearrange("p (h t) -> p h t", h=3), AF.Copy,
                [PB[3]], [B_KTt[tj], PB[3]])
        return f

    def cons_v(l, blkinfo, Vt_fn, B_Vt, out_ap):
        t0, nt, kind, bi = blkinfo

        def f(j, bk):
            si = stg_ctr["n"] % 2
            stg_ctr["n"] += 1
            act(stg[si][:, :], ps[:, bk, 0:384], AF.Copy, [PB[bk]], [B_stg[si], PB[bk]])
            row0 = (t0 + j * 128) if kind == "p" else 0
            P.dma("pool", out_ap[row0:row0 + 128, :], stg[si][:, :], reads=[B_stg[si]], sbuf=B_stg[si])
            tj = (t0 // 128 + j) if kind == "p" else 0
            vop(lambda e: e.tensor_copy(Vt_fn(tj), stg[si][:, :]), [B_stg[si]], [B_Vt[tj]])
        return f

    Pb = [sb("Pb%d" % i, [128, NTB], BF16) for i in range(3)]
    B_Pb = [Buf("Pb%d" % i) for i in range(3)]
    rden = sb("rden", [128, NTB])
    B_rden = Buf("rden")
    pctr = {"n": 0}

    def attn_prompt(l, blk):
        assert 2 * NTB <= 512
        for h in range(6):
            hp, pb = h // 2, 64 * (h % 2)
            ob = 6 + (h % 2)
            ktmax = TPB * blk + TPB - 1
            for kt in range(ktmax + 1):
                q0 = max(0, kt - TPB * blk) * 128
                ncol = NTB - q0
                sbk = 4 + (pctr["n"] % 2)
                pi = pctr["n"] % 3
                pctr["n"] += 1
                jr = [B_QT[j] for j in range(q0 // 128, TPB)]
                mm(ps[:, sbk, q0:NTB], KT[pb:pb + 64, l, hp, kt * 128:(kt + 1) * 128], QT[pb:pb + 64, hp, q0:NTB], True, True,
                   [B_KT[l][kt]] + jr, PB[sbk])
                act(Pb[pi][:, q0:NTB], ps[:, sbk, q0:NTB], AF.Exp, [PB[sbk]], [B_Pb[pi], PB[sbk]], scale=0.125)
                c0 = 128 * (TPB * blk + q0 // 128 - kt + 3)
                vop(lambda e, pi=pi, q0=q0, c0=c0, ncol=ncol: e.tensor_mul(Pb[pi][:, q0:NTB], Pb[pi][:, q0:NTB], mstrip[:, c0:c0 + ncol]),
                    [B_Pb[pi], B_c2], [B_Pb[pi]])
                voff = kt * 384 + h * 64
                mm(ps[0:64, ob, q0:NTB], Vb[:, l, voff:voff + 64], Pb[pi][:, q0:NTB], kt == 0, kt == ktmax, [B_V[l][kt], B_Pb[pi]], PB[ob])
                mm(ps[0:64, ob, NTB + q0:2 * NTB], onesb[:, 0:64], Pb[pi][:, q0:NTB], False, kt == ktmax, [B_const, B_Pb[pi]], PB[ob])
            vop(lambda e, ob=ob: e.reciprocal(rden[0:64, :], ps[0:64, ob, NTB:2 * NTB]), [PB[ob]], [B_rden, PB[ob]])
            vop(lambda e, ob=ob, hp=hp, pb=pb: e.tensor_mul(hm[pb:pb + 64, hp, 0:NTB], ps[0:64, ob, 0:NTB], rden[0:64, :]),
                [PB[ob], B_rden], [B_hm[hp], PB[ob]])

    k_tm = sb("k_tm", [128, TPB, 384], BF16)
    lf_tm = sb("lf_tm", [128, TPB, 384])
    Vh = sb("Vh", [128, TPB, 384], BF16)
    gate = sb("gate", [128, TPB, 384], BF16)
    B_ktm = [Buf("ktm%d" % j) for j in range(TPB)]
    B_lf = [Buf("lf%d" % j) for j in range(TPB)]
    B_Vh = [Buf("Vh%d" % j) for j in range(TPB)]
    B_gate = [Buf("gate%d" % j) for j in range(TPB)]
    qTh = sb("qTh", [64, 6, NTB], BF16)
    kTh = sb("kTh", [64, 6, NTB], BF16)
    B_qTh, B_kTh = Buf("qTh"), Buf("kTh")
    bcT = sb("bcT", [128, 2, NTB], BF16)
    ccT = sb("ccT", [128, 2, NTB])
    uT = sb("uT", [128, 2, max(NTB + 2, 160)])
    uTc = sb("uTc", [128, DEPTH, 2, 2])
    B_bcT = [Buf("bc0"), Buf("bc1")]
    B_ccT = [Buf("cc0"), Buf("cc1")]
    B_uT = [Buf("u0"), Buf("u1")]
    B_uTc = [Buf("uc0"), Buf("uc1")]
    et = [sb("et%d" % i, [128, 384]) for i in range(2)]
    B_et = [Buf("et0"), Buf("et1")]
    onesf = sb("onesf", [128, 1])
    vop(lambda e: e.memset(onesf[:, :], 1.0), [], [B_const], eng="pool")
    vop(lambda e: e.memset(uTc[:, :, :, :], 0.0), [], [B_uTc[0], B_uTc[1]], eng="pool")

    def cons_fr(l, blkinfo):
        def f(j, bk):
            e0 = et[0][:, 0:384]
            act(e0, ps[:, bk, 0:384], AF.Exp, [PB[bk]], [B_et[0], PB[bk]])
            vop(lambda e: e.tensor_scalar_add(e0, e0, 1.0), [B_et[0]], [B_et[0]])
            vop(lambda e: e.reciprocal(e0, e0), [B_et[0]], [B_et[0]])
            vop(lambda e: e.tensor_mul(e0, e0, oml_tm[:, l, :]), [B_et[0], B_c3], [B_et[0]])
            vop(lambda e: e.tensor_copy(k_tm[:, j, :], e0), [B_et[0]], [B_ktm[j]])
            act(lf_tm[:, j, :], e0, AF.Ln, [B_et[0], B_const], [B_lf[j]], scale=-1.0, bias=onesf[:, 0:1])
        return f

    def cons_ir(l, blkinfo):
        def f(j, bk):
            act(Vh[:, j, :], ps[:, bk, 0:384], AF.Copy, [PB[bk]], [B_Vh[j], PB[bk]])
        return f

    def cons_gr(l, blkinfo):
        def f(j, bk):
            e1 = et[1][:, 0:384]
            act(e1, ps[:, bk, 0:384], AF.Exp, [PB[bk]], [B_et[1], PB[bk]], scale=-1.0)
            vop(lambda e: e.tensor_scalar_add(e1, e1, 1.0), [B_et[1]], [B_et[1]])
            vop(lambda e: e.reciprocal(e1, e1), [B_et[1]], [B_et[1]])
            vop(lambda e: e.tensor_mul(e1, e1, nw_tm[:, l, :]), [B_et[1], B_c3], [B_et[1]])
            vop(lambda e: e.tensor_tensor(gate[:, j, :], ps[:, bk, 0:384], e1, ALU.mult), [B_et[1], PB[bk]], [B_gate[j], PB[bk]])
        return f

    def featmajor_groups(l, blkinfo):
        t0, nt, kind, bi = blkinfo
        groups = [("qr", C_QR), ("fr", C_FR), ("cA", C_BC), ("cB", C_BC + 384)]
        for gi, (key, col0) in enumerate(groups):
            wt_, wb = wget("fm", l, gi)
            wt = wt_[:, 0:3072].rearrange("p (a b) -> p a b", a=8)
            for m3 in range(3):
                bk = 1 + (stg_ctr["t"] % 2)
                stg_ctr["t"] += 1
                for k in range(8):
                    mm(ps[:, bk, 0:nt], wt[:, k, m3 * 128:(m3 + 1) * 128], hm[:, k, 0:nt], k == 0, k == 7, [B_hm[k], wb], PB[bk])
                pv = ps[:, bk, 0:nt]
                if key == "qr":
                    for hf in range(2):
                        act(qTh[:, 2 * m3 + hf, 0:nt], ps[64 * hf:64 * hf + 64, bk, 0:nt], AF.Copy, [PB[bk]], [B_qTh, PB[bk]])
                elif key == "fr":
                    e0 = et[0][:, 0:nt]
                    act(e0, pv, AF.Exp, [PB[bk]], [B_et[0], PB[bk]])
                    vop(lambda e, e0=e0: e.tensor_scalar_add(e0, e0, 1.0), [B_et[0]], [B_et[0]])
                    vop(lambda e, e0=e0: e.reciprocal(e0, e0), [B_et[0]], [B_et[0]])
                    vop(lambda e, e0=e0, m3=m3: e.tensor_scalar_mul(e0, e0, oml_fm[:, l, m3:m3 + 1]), [B_et[0], B_c3], [B_et[0]])
                    for hf in range(2):
                        act(kTh[:, 2 * m3 + hf, 0:nt], et[0][64 * hf:64 * hf + 64, 0:nt], AF.Copy, [B_et[0]], [B_kTh])
                else:
                    ci = (0 if key == "cA" else 3) + m3
                    if ci < 2:
                        act(bcT[:, ci, 0:nt], pv, AF.Copy, [PB[bk]], [B_bcT[ci], PB[bk]])
                    elif ci < 4:
                        act(ccT[:, ci - 2, 0:nt], pv, AF.Copy, [PB[bk]], [B_ccT[ci - 2], PB[bk]])
                    else:
                        mi = ci - 4
                        if kind == "p":
                            dst = uT[:, mi, 2:2 + nt]
                            vop(lambda e, dst=dst, pv=pv, mi=mi: e.tensor_tensor(dst, pv, ccT[:, mi, 0:nt], ALU.mult),
                                [PB[bk], B_ccT[mi]], [B_uT[mi], PB[bk]])
                        else:
                            dst = uT[:, mi, 0:160].rearrange("p (b t) -> p b t", b=16)[:, :, 2:10]
                            vop(lambda e, dst=dst, pv=pv, mi=mi: e.tensor_tensor(
                                dst, pv.rearrange("p (b t) -> p b t", b=16), ccT[:, mi, 0:128].rearrange("p (b t) -> p b t", b=16), ALU.mult),
                                [PB[bk], B_ccT[mi]], [B_uT[mi], PB[bk]])

    cacc = [sb("cacc%d" % i, [128, NTB]) for i in range(2)]
    B_cacc = [Buf("cacc0"), Buf("cacc1")]

    def conv_pre(l, blkinfo):
        t0, nt, kind, bi = blkinfo
        for mi in range(2):
            if kind == "p":
                vop(lambda e, mi=mi: e.tensor_copy(uT[:, mi, 0:2], uTc[:, l, mi, :]), [B_uTc[l]], [B_uT[mi]])
            else:
                for b in range(16):
                    P.dma("sp", uT[:, mi, b * 10:b * 10 + 2], I["sc"][l][b, :, mi * 128:(mi + 1) * 128].rearrange("j p -> p j"),
                          writes=[B_uT[mi]], sbuf=B_uT[mi], allow_slow_non_contiguous=True)

    def conv_block(l, blkinfo):
        t0, nt, kind, bi = blkinfo
        for mi in range(2):
            a = cacc[mi]
            if kind == "p":
                def uv(sh, mi=mi):
                    return uT[:, mi, sh:sh + nt]
                av = a[:, 0:nt]
                bcv = bcT[:, mi, 0:nt]
                ov = hm[:, 6 + mi, 0:nt]
            else:
                def uv(sh, mi=mi):
                    return uT[:, mi, 0:160].rearrange("p (b t) -> p b t", b=16)[:, :, sh:sh + 8]
                av = a[:, 0:128].rearrange("p (b t) -> p b t", b=16)
                bcv = bcT[:, mi, 0:128].rearrange("p (b t) -> p b t", b=16)
                ov = hm[:, 6 + mi, 0:128].rearrange("p (b t) -> p b t", b=16)
            w = lambda tap, mi=mi: cw_fm[:, l, mi, tap:tap + 1]
            vop(lambda e, uv=uv, av=av, w=w, mi=mi: e.tensor_scalar(av, uv(2), w(2), cb_fm[:, l, mi:mi + 1], op0=ALU.mult, op1=ALU.add),
                [B_uT[mi], B_c3], [B_cacc[mi]])
            vop(lambda e, uv=uv, av=av, w=w: e.scalar_tensor_tensor(av, uv(1), w(1), av, op0=ALU.mult, op1=ALU.add),
                [B_uT[mi], B_c3, B_cacc[mi]], [B_cacc[mi]])
            vop(lambda e, uv=uv, av=av, w=w: e.scalar_tensor_tensor(av, uv(0), w(0), av, op0=ALU.mult, op1=ALU.add),
                [B_uT[mi], B_c3, B_cacc[mi]], [B_cacc[mi]])
            vop(lambda e, av=av, bcv=bcv, ov=ov: e.tensor_mul(ov, av, bcv), [B_cacc[mi], B_bcT[mi]], [B_hm[6 + mi]])
            if kind == "p":
                vop(lambda e, mi=mi: e.tensor_copy(uTc[:, l, mi, :], uT[:, mi, nt:nt + 2]), [B_uT[mi]], [B_uTc[l]])
                if bi == NPB - 1:
                    P.dma("sp", O["cvp"][l][:, mi * 128:(mi + 1) * 128].rearrange("j p -> p j"), uT[:, mi, nt:nt + 2],
                          reads=[B_uT[mi]], sbuf=B_uT[mi], allow_slow_non_contiguous=True)
            else:
                for b in range(16):
                    P.dma("sp", O["cvs"][l][b, :, mi * 128:(mi + 1) * 128].rearrange("j p -> p j"),
                          uT[:, mi, b * 10 + 8:b * 10 + 10], reads=[B_uT[mi]], sbuf=B_uT[mi], allow_slow_non_contiguous=True)

    Sst = sb("Sst", [64, DEPTH, 6, 64])
    Sbf = sb("Sbf", [64, DEPTH, 6, 64], BF16)
    B_S = [Buf("S0"), Buf("S1")]
    B_Sbf = [Buf("Sbf0"), Buf("Sbf1")]
    eb = sb("eb", [64, 6, 128])
    enb = sb("enb", [64, 6, 128])
    B_eb, B_enb = Buf("eb"), Buf("enb")
    QZN = 6 * GP * 128
    Qz = sb("Qz", [64, QZN], BF16)
    B_Qz = Buf("Qz")
    KtT = sb("KtT", [64, 6, 128], BF16)
    B_KtT = Buf("KtT")
    Khat = sb("Khat", [128, 384], BF16)
    B_Khat = Buf("Khat")
    Khz = sb("Khz", [128, 4, 384], BF16)
    B_Khz = Buf("Khz")
    Am = sb("Am", [128, 6, 128], BF16)
    B_Am = Buf("Am")
    osq = sb("osq", [128, 384])
    B_osq = Buf("osq")
    oss = sb("oss", [128, 8])
    B_oss = Buf("oss")
    onb = sb("onb", [128, 384], BF16)
    B_onb = Buf("onb")
    Vz = sb("Vz", [128, 16, 64], BF16)
    B_Vz = Buf("Vz")
    S0 = sb("S0", [64, 16, 64])
    S0b = sb("S0b", [64, 16, 64], BF16)
    B_S0, B_S0b = Buf("S0s"), Buf("S0bs")
    vop(lambda e: e.memset(Sst[:, :, :, :], 0.0), [], [B_S[0], B_S[1]], eng="pool")
    vop(lambda e: e.memset(Sbf[:, :, :, :], 0.0), [], [B_Sbf[0], B_Sbf[1]], eng="pool")
    vop(lambda e: e.memset(Qz[:, :], 0.0), [], [B_Qz], eng="pool")

    def hgrn_tile(l, blkinfo, j):
        t0, nt, kind, bi = blkinfo
        C, G = (CP, GP) if kind == "p" else (CS, GS)
        tri, uu, trib = (triP, uP, triPb) if kind == "p" else (triS, uS, triSb)
        tc0 = j * 128
        for h in range(6):
            bk = 4 if h < 4 else 5
            mm(ps[0:64, bk, (h % 4) * 128:(h % 4 + 1) * 128], lf_tm[:, j, h * 64:(h + 1) * 64], tri[:, :], True, True,
               [B_lf[j], B_const], PB[bk])
        act(eb[:, 0:4, :], ps[0:64, 4, :].rearrange("p (h t) -> p h t", h=4), AF.Exp, [PB[4]], [B_eb, PB[4]])
        act(enb[:, 0:4, :], ps[0:64, 4, :].rearrange("p (h t) -> p h t", h=4), AF.Exp, [PB[4]], [B_enb, PB[4]], scale=-1.0)
        act(eb[:, 4:6, :], ps[0:64, 5, 0:256].rearrange("p (h t) -> p h t", h=2), AF.Exp, [PB[5]], [B_eb, PB[5]])
        act(enb[:, 4:6, :], ps[0:64, 5, 0:256].rearrange("p (h t) -> p h t", h=2), AF.Exp, [PB[5]], [B_enb, PB[5]], scale=-1.0)
        vop(lambda e: e.tensor_mul(KtT[:, :, :], kTh[:, :, tc0:tc0 + 128], enb[:, :, :]), [B_kTh, B_enb], [B_KtT])
        mm(ps[:, 6, 0:384], uu[:, :], lf_tm[:, j, :], True, True, [B_lf[j], B_const], PB[6])
        act(osq[:, :], ps[:, 6, 0:384], AF.Exp, [PB[6]], [B_osq, PB[6]])
        vop(lambda e: e.tensor_mul(Khat[:, :], k_tm[:, j, :], osq[:, :]), [B_ktm[j], B_osq], [B_Khat])
        if kind == "p":
            qz4 = sap(Qz, 0, [[QZN, 64], [G * 128, 6], [128 + C, G], [1, C]])
            vop(lambda e: e.tensor_mul(qz4, qTh[:, :, tc0:tc0 + 128].rearrange("p h (g c) -> p h g c", g=G),
                                       eb[:, :, :].rearrange("p h (g c) -> p h g c", g=G)), [B_qTh, B_eb], [B_Qz])
            for g in range(G):
                vop(lambda e, g=g: e.tensor_scalar_mul(Khz[:, g, :], Khat[:, :], rowmP[:, g:g + 1]), [B_Khat, B_c2], [B_Khz])
            for h in range(6):
                bk = 4 if h < 4 else 5
                qfull = sap(Qz, h * G * 128, [[QZN, 64], [128 + C, G], [1, C]])
                mm(ps[:, bk, (h % 4) * 128:(h % 4 + 1) * 128], KtT[:, h, :], qfull, True, True, [B_Qz, B_KtT], PB[bk])
            vop(lambda e: e.tensor_tensor(Am[:, 0:4, :], ps[:, 4, :].rearrange("p (h t) -> p h t", h=4),
                                          sap(trib, 0, [[128, 128], [0, 4], [1, 128]]), ALU.mult), [PB[4], B_c2], [B_Am, PB[4]])
            vop(lambda e: e.tensor_tensor(Am[:, 4:6, :], ps[:, 5, 0:256].rearrange("p (h t) -> p h t", h=2),
                                          sap(trib, 0, [[128, 128], [0, 2], [1, 128]]), ALU.mult), [PB[5], B_c2], [B_Am, PB[5]])
            first = True
            for h in range(6):
                mm(ps[:, 7, h * 64:(h + 1) * 64], Am[:, h, :], Vh[:, j, h * 64:(h + 1) * 64], first, False, [B_Am, B_Vh[j]], PB[7])
                first = False
            for g in range(G):
                for h in range(6):
                    mm(ps[:, 7, h * 64:(h + 1) * 64], Qz[:, (h * G + g) * 128:(h * G + g + 1) * 128], Sbf[:, l, h, :], False, False,
                       [B_Qz, B_Sbf[l]], PB[7])
                for h in range(6):
                    mm(ps[0:64, 6, h * 64:(h + 1) * 64], Khz[:, g, h * 64:(h + 1) * 64], Vh[:, j, h * 64:(h + 1) * 64], h == 0, True,
                       [B_Khz, B_Vh[j]], PB[6])
                ebl = sap(eb, g * C + C - 1, [[6 * 128, 64], [128, 6], [0, 64]])
                vop(lambda e, ebl=ebl: e.tensor_mul(Sst[:, l, :, :], Sst[:, l, :, :], ebl), [B_S[l], B_eb], [B_S[l]])
                vop(lambda e: e.tensor_add(Sst[:, l, :, :], Sst[:, l, :, :], ps[0:64, 6, 0:384].rearrange("p (h v) -> p h v", h=6)),
                    [B_S[l], PB[6]], [B_S[l], PB[6]])
                act(Sbf[:, l, :, :], Sst[:, l, :, :], AF.Copy, [B_S[l]], [B_Sbf[l]])
        else:
            for h in range(6):
                P.dma("sp", S0[:, :, :], I["sh"][l][:, h, :, :].rearrange("b k v -> k b v"), writes=[B_S0], sbuf=B_S0)
                act(S0b[:, :, :], S0[:, :, :], AF.Copy, [B_S0], [B_S0b])
                qz3 = sap(Qz, 0, [[QZN, 64], [128 + C, G], [1, C]])
                vop(lambda e, h=h, qz3=qz3: e.tensor_mul(qz3, qTh[:, h, 0:128].rearrange("p (g c) -> p g c", g=G),
                                                         eb[:, h, :].rearrange("p (g c) -> p g c", g=G)), [B_qTh, B_eb], [B_Qz])
                mm(ps[:, 4, 0:128], KtT[:, h, :], qz3, True, True, [B_Qz, B_KtT], PB[4])
                vop(lambda e, h=h: e.tensor_tensor(Am[:, h, :], ps[:, 4, 0:128], trib[:, :], ALU.mult), [PB[4], B_c2], [B_Am, PB[4]])
                mm(ps[:, 7, h * 64:(h + 1) * 64], Am[:, h, :], Vh[:, j, h * 64:(h + 1) * 64], h == 0, False, [B_Am, B_Vh[j]], PB[7])
                for b in range(16):
                    mm(ps[:, 7, h * 64:(h + 1) * 64], Qz[:, b * 128:(b + 1) * 128], S0b[:, b, :], False, False, [B_Qz, B_S0b], PB[7])
                vop(lambda e, h=h: e.tensor_tensor(Vz[:, :, :], sap(Vh, j * 384 + h * 64, [[TPB * 384, 128], [0, 16], [1, 64]]),
                                                   sap(rowmS, 0, [[GS, 128], [1, 16], [0, 64]]), ALU.mult), [B_Vh[j], B_c2], [B_Vz])
                for half in range(2):
                    mm(ps[0:64, 5, 0:512], Khat[:, h * 64:(h + 1) * 64], Vz[:, half * 8:half * 8 + 8, :].rearrange("p b v -> p (b v)"),
                       True, True, [B_Khat, B_Vz], PB[5])
                    ebl = sap(eb, h * 128 + half * 64 + C - 1, [[6 * 128, 64], [C, 8], [0, 64]])
                    vop(lambda e, half=half, ebl=ebl: e.tensor_mul(S0[:, half * 8:half * 8 + 8, :], S0[:, half * 8:half * 8 + 8, :], ebl),
                        [B_S0, B_eb], [B_S0])
                    vop(lambda e, half=half: e.tensor_add(S0[:, half * 8:half * 8 + 8, :], S0[:, half * 8:half * 8 + 8, :],
                                                          ps[0:64, 5, 0:512].rearrange("p (b v) -> p b v", b=8)),
                        [B_S0, PB[5]], [B_S0, PB[5]])
                P.dma("sp", O["hs"][l][:, h, :, :].rearrange("b k v -> k b v"), S0[:, :, :], reads=[B_S0], sbuf=B_S0)
        act(osq[:, :], ps[:, 7, 0:384], AF.Square, [PB[7]], [B_osq, PB[7]])
        vop(lambda e: e.tensor_reduce(oss[:, 0:6], osq[:, :].rearrange("p (h v) -> p h v", h=6), AX.X, ALU.add), [B_osq], [B_oss])
        act(oss[:, 0:6], oss[:, 0:6], AF.Ln, [B_oss, B_const], [B_oss], scale=1.0 / 64, bias=epsb[:, 0:1])
        act(oss[:, 0:6], oss[:, 0:6], AF.Exp, [B_oss], [B_oss], scale=-0.5)
        vop(lambda e: e.tensor_tensor(osq[:, :].rearrange("p (h v) -> p h v", h=6), ps[:, 7, 0:384].rearrange("p (h v) -> p h v", h=6),
                                      sap(oss, 0, [[8, 128], [1, 6], [0, 64]]), ALU.mult), [PB[7], B_oss], [B_osq, PB[7]])
        vop(lambda e: e.tensor_mul(onb[:, :], osq[:, :], gate[:, j, :]), [B_osq, B_gate[j]], [B_onb])
        for m3 in range(3):
            tr(psbf(3)[:, m3 * 128:(m3 + 1) * 128], onb[:, m3 * 128:(m3 + 1) * 128], identb[:, :], [B_onb, B_c2], PB[3])
        act(hm[:, 3:6, tc0:tc0 + 128], psbf(3)[:, 0:384].rearrange("p (h t) -> p h t", h=3), AF.Copy,
            [PB[3]], [B_hm[3], B_hm[4], B_hm[5], PB[3]])

    def out_proj(l, blkinfo):
        t0, nt, kind, bi = blkinfo
        for c in range(4):
            wt_, wb = wget("wo", l, c)
            wt = wt_[:, 0:2048].rearrange("p (a b) -> p a b", a=8)
            for m2 in range(2):
                m = c * 2 + m2
                bk = 1 + (m % 2)
                for k in range(8):
                    mm(ps[:, bk, 0:nt], wt[:, k, m2 * 128:(m2 + 1) * 128], hm[:, k, 0:nt], k == 0, k == 7, [wb, B_hm[k]], PB[bk])
                resid_update(l, blkinfo, m, bk, 16)

    actT = sb("actT", [128, NF, NTB], BF16)
    B_act = [Buf("act%d" % f) for f in range(NF)]

    def ffn(l, blkinfo):
        t0, nt, kind, bi = blkinfo
        norm_block(l, blkinfo, 1)
        for f in range(NF):
            wt, wb = wget("fi", l, f)
            gb, ub = (4, 5) if f % 2 == 0 else (6, 7)
            for k in range(8):
                mm(ps[:, gb, 0:nt], wt[:, k * 128:(k + 1) * 128], hm[:, k, 0:nt], k == 0, k == 7, [wb, B_hm[k]], PB[gb])
            for k in range(8):
                mm(ps[:, ub, 0:nt], wt[:, 1024 + k * 128:1024 + (k + 1) * 128], hm[:, k, 0:nt], k == 0, k == 7, [wb, B_hm[k]], PB[ub])
            s = f % 2
            act(et[s][:, 0:nt], ps[:, gb, 0:nt], AF.Silu, [PB[gb]], [B_et[s], PB[gb]])
            vop(lambda e, f=f, s=s, ub=ub: e.tensor_tensor(actT[:, f, 0:nt], ps[:, ub, 0:nt], et[s][:, 0:nt], ALU.mult),
                [PB[ub], B_et[s]], [B_act[f], PB[ub]])
        for m in range(8):
            wt_, wb = wget("fo", l, m)
            wt = wt_[:, 0:NF * 128].rearrange("p (a b) -> p a b", a=NF)
            bk = 1 + (m % 2)
            for f in range(NF):
                mm(ps[:, bk, 0:nt], wt[:, f, :], actT[:, f, 0:nt], f == 0, f == NF - 1, [wb, B_act[f]], PB[bk])
            resid_update(l, blkinfo, m, bk, 40)

    def final_block(blkinfo):
        t0, nt, kind, bi = blkinfo
        ssq_rstd(nt)
        ys = 1
        for j in range(nt // 128):
            for half in range(2):
                bk = 4 + half
                for kk in range(4):
                    k = half * 4 + kk
                    s = k % 2
                    vop(lambda e, k=k, s=s, j=j: e.tensor_mul(ntmp[s][:, 0:128], xTb[:, k, j * 128:(j + 1) * 128], rstd[:, j * 128:(j + 1) * 128]),
                        [B_xT[k], B_rstd], [B_ntmp[s]])
                    act(ntmp[s][:, 128:256], ntmp[s][:, 0:128], AF.Copy, [B_ntmp[s], B_c3], [B_ntmp[s]], scale=fnw_fm[:, k:k + 1])
                    tr(ps[:, bk, kk * 128:(kk + 1) * 128], ntmp[s][:, 128:256], ident[:, :], [B_ntmp[s], B_const], PB[bk])
                if half == 0:
                    act(xst[ys][:, 0:512], ps[:, bk, :], AF.Copy, [PB[bk]], [B_xst[ys], PB[bk]])
                else:
                    vop(lambda e, bk=bk: e.tensor_copy(xst[ys][:, 512:1024], ps[:, bk, :]), [PB[bk]], [B_xst[ys], PB[bk]])
            dst = O["yp"][t0 + j * 128:t0 + (j + 1) * 128, :] if kind == "p" else O["ys"]
            P.dma("pool", dst, xst[ys][:, :], reads=[B_xst[ys]], sbuf=B_xst[ys])

    KTs = sb("KTs", [128, 3, 128], BF16)
    Vs = sb("Vs", [128, 384], BF16)
    B_KTs, B_Vs = [Buf("KTs")], [Buf("Vs")]
    Qblk = sb("Qblk", [128, 3, 16, 2, 8], BF16)
    B_Qblk = Buf("Qblk")
    Kcs = [KT[:, 0, :, :].rearrange("p a b -> p (a b)").rearrange("p (r f) -> p r f", r=16),
           Vb[:, 1, :].rearrange("p (r f) -> p r f", r=16)]
    KcT = KT[:, 1, :, :].rearrange("p a b -> p (a b)").rearrange("p (i t) -> p i t", t=128)
    Vc = Vb[:, 0, :].rearrange("p (r f) -> p r f", r=16)
    B_Kcs, B_Vc = [Buf("Kc0"), Buf("Kc1")], Buf("Vc")
    B_KcT = [Buf("KcT%d" % i) for i in range(6)]
    Ps = sb("Ps", [128, 768], BF16)
    B_Ps = Buf("Ps")
    Pn = sb("Pn", [128, 256], BF16)
    B_Pn = Buf("Pn")
    rds = sb("rds", [128, 768])
    B_rds = Buf("rds")

    def attn_sample(l):
        vop(lambda e: e.memset(Qblk[:, :, :, :, :], 0.0), [], [B_Qblk])
        for i in range(2):
            vop(lambda e, i=i: e.memset(Kcs[i][0:96, 8:16, :], 0.0), [], [B_Kcs[i]], eng="pool")
        vop(lambda e: e.memset(Vc[0:96, 8:16, :], 0.0), [], [B_Vc], eng="pool")
        for h2 in range(2):
            vop(lambda e, h2=h2: e.tensor_copy(Qblk[64 * h2:64 * h2 + 64, :, :, h2, :],
                                               QT[64 * h2:64 * h2 + 64, :, 0:128].rearrange("p h (b q) -> p h b q", b=16)),
                [B_QT[0]], [B_Qblk])
        for b in range(16):
            Kc, B_Kc = Kcs[b % 2], B_Kcs[b % 2]
            ckv = I["ck"][l][b].rearrange("(g r) f -> g r f", r=16)
            cvv = I["cv"][l][b].rearrange("(g r) f -> g r f", r=16)
            P.dma("pool", Kc[0:96, 0:8, :], ckv[0:96, 0:8, :], writes=[B_Kc], sbuf=B_Kc)
            P.dma("pool", Kc[96:128, :, :], ckv[96:128, :, :], writes=[B_Kc], sbuf=B_Kc)
            P.dma("pool", Vc[0:96, 0:8, :], cvv[0:96, 0:8, :], writes=[B_Vc], sbuf=B_Vc)
            P.dma("pool", Vc[96:128, :, :], cvv[96:128, :, :], writes=[B_Vc], sbuf=B_Vc)
            for grp in range(6):
                bk = 4 + (grp % 2)
                for i in range(8):
                    idx = grp * 8 + i
                    r, hp = idx // 3, idx % 3
                    tr(psbf(bk)[:, i * 128:(i + 1) * 128], Kc[:, r, hp * 128:(hp + 1) * 128], identb[:, :], [B_Kc, B_c2], PB[bk])
                if grp % 2 == 0:
                    act(KcT[:, grp * 8:grp * 8 + 8, :], psbf(bk)[:, :].rearrange("p (i t) -> p i t", i=8), AF.Copy, [PB[bk]], [B_KcT[grp], PB[bk]])
                else:
                    vop(lambda e, grp=grp, bk=bk: e.tensor_copy(KcT[:, grp * 8:grp * 8 + 8, :], psbf(bk)[:, :].rearrange("p (i t) -> p i t", i=8)),
                        [PB[bk]], [B_KcT[grp], PB[bk]])
            for r in range(16):
                bk = 6 + r // 8
                for hp in range(3):
                    idx = r * 3 + hp
                    c0 = ((r % 8) * 3 + hp) * 16
                    mm(ps[:, bk, c0:c0 + 16], KcT[:, idx, :], Qblk[:, hp, b, :, :].rearrange("p a q -> p (a q)"), True, True,
                       [B_KcT[idx // 8], B_Qblk], PB[bk])
            act(Ps[:, 0:384], ps[:, 6, 0:384], AF.Exp, [PB[6]], [B_Ps, PB[6]], scale=0.125)
            act(Ps[:, 384:768], ps[:, 7, 0:384], AF.Exp, [PB[7]], [B_Ps, PB[7]], scale=0.125)
            vop(lambda e: e.tensor_tensor(Ps[:, :].rearrange("p (r a q) -> p r a q", r=16, a=6),
                                          Ps[:, :].rearrange("p (r a q) -> p r a q", r=16, a=6),
                                          sap(mc, 0, [[128, 128], [8, 16], [0, 6], [1, 8]]), ALU.mult), [B_Ps, B_c2], [B_Ps])
            for hp in range(3):
                ob = 0 if hp < 2 else 1
                oc = (hp % 2) * 256 + b * 16
                for r in range(16):
                    first = (b == 0 and r == 0 and hp in (0, 2))
                    mm(ps[:, ob, oc:oc + 16], Vc[:, r, hp * 128:(hp + 1) * 128], Ps[:, (r * 3 + hp) * 16:(r * 3 + hp) * 16 + 16], first, False,
                       [B_Vc, B_Ps], PB[ob])
            db = 2 if b < 10 else 3
            dc = (b if b < 10 else b - 10) * 48
            for r in range(16):
                first = (r == 0 and b in (0, 10))
                mm(ps[:, db, dc:dc + 48], onesb[:, :], Ps[:, r * 48:(r + 1) * 48], first, False, [B_const, B_Ps], PB[db])
        vop(lambda e: e.tensor_copy(rds[:, 0:480], ps[:, 2, 0:480]), [PB[2]], [B_rds, PB[2]])
        vop(lambda e: e.tensor_copy(rds[:, 480:768], ps[:, 3, 0:288]), [PB[3]], [B_rds, PB[3]])
        for hp in range(3):
            mm(ps[:, 4, 0:256], KTs[:, hp, :], Qblk[:, hp, :, :, :].rearrange("p b a q -> p (b a q)"), True, True, [B_KTs[0], B_Qblk], PB[4])
            act(Pn[:, :], ps[:, 4, 0:256], AF.Exp, [PB[4]], [B_Pn, PB[4]], scale=0.125)
            vop(lambda e: e.tensor_tensor(Pn[:, :].rearrange("p (b a q) -> p b a q", b=16, a=2),
                                          Pn[:, :].rearrange("p (b a q) -> p b a q", b=16, a=2),
                                          sap(mnew, 0, [[128, 128], [8, 16], [0, 2], [1, 8]]), ALU.mult), [B_Pn, B_c2], [B_Pn])
            ob = 0 if hp < 2 else 1
            oc = (hp % 2) * 256
            mm(ps[:, ob, oc:oc + 256], Vs[:, hp * 128:(hp + 1) * 128], Pn[:, :], False, True, [B_Vs[0], B_Pn], PB[ob])
            mm(ps[:, 5, 0:256], onesb[:, :], Pn[:, :], True, True, [B_const, B_Pn], PB[5])
            rv = rds[:, :].rearrange("p (b x a q) -> p b x a q", b=16, x=3, a=2)[:, :, hp, :, :]
            vop(lambda e, rv=rv: e.tensor_tensor(rv, rv, ps[:, 5, 0:256].rearrange("p (b a q) -> p b a q", b=16, a=2), ALU.add),
                [B_rds, PB[5]], [B_rds, PB[5]])
        vop(lambda e: e.reciprocal(rds[:, :], rds[:, :]), [B_rds], [B_rds])
        for hp in range(3):
            ob = 0 if hp < 2 else 1
            oc = (hp % 2) * 256
            for h2 in range(2):
                pb = 64 * h2
                ov = ps[pb:pb + 64, ob, oc:oc + 256].rearrange("p (b a q) -> p b a q", b=16, a=2)[:, :, h2, :]
                rv = rds[pb:pb + 64, :].rearrange("p (b x a q) -> p b x a q", b=16, x=3, a=2)[:, :, hp, h2, :]
                vop(lambda e, ov=ov, rv=rv, hp=hp, pb=pb: e.tensor_tensor(hm[pb:pb + 64, hp, 0:128].rearrange("p (b q) -> p b q", b=16), ov, rv, ALU.mult),
                    [PB[ob], B_rds], [B_hm[hp], PB[ob]])

    nblocks = len(BLOCKS) if stage >= 99 else min(len(BLOCKS), stage)
    for blkinfo in BLOCKS[:nblocks] if stage >= 99 or stage < 50 else BLOCKS:
        t0, nt, kind, bi = blkinfo
        if kind == "s":
            P.barrier()
            vop(lambda e: e.memset(Qz[:, :], 0.0), [], [B_Qz])
        load_x(blkinfo)
        for l in range(DEPTH):
            norm_block(l, blkinfo, 0)
            if kind == "p":
                tokmajor_group(l, blkinfo, C_QA, "qa", cons_q(l, blkinfo))
                tokmajor_group(l, blkinfo, C_KA, "ka",
                               cons_k(l, blkinfo, lambda tj, l=l: KT[:, l, :, tj * 128:(tj + 1) * 128], B_KT[l], O["kp"][l]))
                tokmajor_group(l, blkinfo, C_VA, "va",
                               cons_v(l, blkinfo, lambda tj, l=l: Vb[:, l, tj * 384:(tj + 1) * 384], B_V[l], O["vp"][l]))
            else:
                tokmajor_group(l, blkinfo, C_QA, "qa", cons_q(l, blkinfo))
                tokmajor_group(l, blkinfo, C_KA, "ka", cons_k(l, blkinfo, lambda tj: KTs[:, :, :], B_KTs, O["ks"][l]))
                tokmajor_group(l, blkinfo, C_VA, "va", cons_v(l, blkinfo, lambda tj: Vs[:, :], B_Vs, O["vs"][l]))
            tokmajor_group(l, blkinfo, C_FR, "fr", cons_fr(l, blkinfo))
            tokmajor_group(l, blkinfo, C_IR, "ir", cons_ir(l, blkinfo))
            tokmajor_group(l, blkinfo, C_GR, "gr", cons_gr(l, blkinfo))
            conv_pre(l, blkinfo)
            featmajor_groups(l, blkinfo)
            if kind == "p":
                attn_prompt(l, bi)
            else:
                attn_sample(l)
            conv_block(l, blkinfo)
            for j in range(nt // 128):
                hgrn_tile(l, blkinfo, j)
            if kind == "p" and bi == NPB - 1:
                P.dma("sp", O["hp"][l].rearrange("h k v -> k h v"), Sst[:, l, :, :], reads=[B_S[l]], sbuf=B_S[l])
            out_proj(l, blkinfo)
            ffn(l, blkinfo)
        final_block(blkinfo)

    P.wait_dmas("sp")
    P.emit()
    es.close()
    P.close()
    return nc, P.stats


DEFAULT_STAGE = 99
_CACHE = {}


def kernel(**inputs):
    x_prompt = np.asarray(inputs["x_prompt"], np.float32)
    consts = host_consts()
    stage = int(inputs.pop("_stage")) if "_stage" in inputs else DEFAULT_STAGE
    if "prog" not in _CACHE:
        _CACHE["prog"] = build_program(stage)
    nc, stats = _CACHE["prog"]
    in_maps = []
    g = lambda n: np.asarray(inputs[n], np.float32)
    for c in range(8):
        m = {}
        m["xp"] = np.ascontiguousarray(g("x_prompt")[c])
        m["xs"] = np.ascontiguousarray(g("x_sample")[16 * c:16 * c + 16].reshape(128, D))
        m["call"] = np.ascontiguousarray(np.concatenate([g("c_prompt")[c:c + 1], g("c_sample")[16 * c:16 * c + 16]], axis=0))
        m["ck"] = np.ascontiguousarray(g("cache_k")[:, 16 * c:16 * c + 16].reshape(DEPTH, 16, 2048, 384))
        m["cv"] = np.ascontiguousarray(g("cache_v")[:, 16 * c:16 * c + 16].reshape(DEPTH, 16, 2048, 384))
        m["sh"] = np.ascontiguousarray(g("state_hgrn")[:, 16 * c:16 * c + 16])
        m["sc"] = np.ascontiguousarray(g("state_conv")[:, 16 * c:16 * c + 16])
        m["w_ada"] = g("w_ada")
        m["b_ada"] = g("b_ada").reshape(DEPTH, 48, 128)
        m["w_in"] = g("w_in")
        m["conv_w"] = g("conv_w")
        m["conv_b"] = g("conv_b")
        m["lbl"] = g("hgrn_lb_logits")
        m["nw"] = g("hgrn_norm_w")
        m["w_out"] = g("w_out")
        m["w_ffn_in"] = g("w_ffn_in")
        m["w_ffn_out"] = g("w_ffn_out")
        m["fnw"] = g("final_norm_w").reshape(8, 128)
        for n, v in consts.items():
            m["c_" + n] = v
        in_maps.append(m)
    res = run_bass_kernel_spmd(nc, in_maps, core_ids=list(range(8)))
    R = res.results
    cat = lambda n: np.stack([R[c][n] for c in range(8)], axis=0)
    y_p = cat("yp")
    y_s = np.concatenate([R[c]["ys"].reshape(16, 8, D) for c in range(8)], axis=0)
    k_p = np.stack([R[c]["kp"] for c in range(8)], axis=1).reshape(DEPTH, 8, SEQ, 6, 64)
    v_p = np.stack([R[c]["vp"] for c in range(8)], axis=1).reshape(DEPTH, 8, SEQ, 6, 64)
    h_p = np.stack([R[c]["hp"] for c in range(8)], axis=1)
    cv_p = np.stack([R[c]["cvp"] for c in range(8)], axis=1)
    k_s = np.concatenate([R[c]["ks"].reshape(DEPTH, 16, 8, 6, 64) for c in range(8)], axis=1)
    v_s = np.concatenate([R[c]["vs"].reshape(DEPTH, 16, 8, 6, 64) for c in range(8)], axis=1)
    h_s = np.concatenate([R[c]["hs"] for c in range(8)], axis=1)
    cv_s = np.concatenate([R[c]["cvs"] for c in range(8)], axis=1)
    return (y_p, y_s, k_p, v_p, h_p, cv_p, k_s, v_s, h_s, cv_s)
```

```python
import numpy as np
from contextlib import ExitStack
import concourse.bass as bass
import concourse.mybir as mybir
from concourse.bass_utils import run_bass_kernel_spmd

F32 = mybir.dt.float32
BF16 = mybir.dt.bfloat16
AF = mybir.ActivationFunctionType
ALU = mybir.AluOpType
AX = mybir.AxisListType

ENGS = ("pe", "act", "dve", "pool", "sp")


class Buf:
    __slots__ = ("name", "w", "r", "dsem", "dcount")

    def __init__(self, name=""):
        self.name = name
        self.w = None
        self.r = []
        self.dsem = None
        self.dcount = 0


class Op:
    __slots__ = ("eng", "idx", "fn", "waits", "dwaits", "sig", "clock", "is_dma", "dtok", "trig")

    def __init__(self, eng, idx, fn):
        self.eng = eng
        self.idx = idx
        self.fn = fn
        self.waits = {}
        self.dwaits = []
        self.sig = False
        self.clock = None
        self.is_dma = False
        self.dtok = None
        self.trig = False


class Prog:
    def __init__(self, nc):
        self.nc = nc
        self.q = {e: [] for e in ENGS}
        self.known = {e: {f: -1 for f in ENGS} for e in ENGS}
        self.dknown = {e: {} for e in ENGS}
        self._semctx = []
        self.dbufs = []
        self.dsems = []

    def new_sem(self, name):
        cm = self.nc.semaphore(name)
        s = cm.__enter__()
        self._semctx.append(cm)
        return s

    def close(self):
        for cm in reversed(self._semctx):
            cm.__exit__(None, None, None)
        self._semctx = []

    def _need_dtok(self, op, tok):
        sem, val = tok
        kd = self.dknown[op.eng]
        if kd.get(id(sem), -1) < val:
            kd[id(sem)] = val
            op.dwaits.append((sem, val))

    def _deps(self, op, reads, writes):
        eng = op.eng
        need = []
        for b in reads:
            if b.w is not None:
                need.append((b.w, "raw"))
        for b in writes:
            if b.w is not None:
                need.append((b.w, "waw"))
            for r in b.r:
                need.append((r, "war"))
        dmax = {}
        for d, kind in need:
            if d.is_dma:
                sem, val = d.dtok
                if id(sem) not in dmax or dmax[id(sem)][1] < val:
                    dmax[id(sem)] = (sem, val)
        for tok in dmax.values():
            self._need_dtok(op, tok)
        for d, kind in need:
            if d.is_dma:
                continue
            if d.eng == eng:
                if eng in ("pe", "sp"):
                    continue
            if self.known[eng][d.eng] >= d.idx:
                continue
            cur = op.waits.get(d.eng)
            if cur is None or cur.idx < d.idx:
                op.waits[d.eng] = d
        k = self.known[eng]
        for f, d in op.waits.items():
            d.sig = True
            if d.clock is not None:
                for g, v in d.clock.items():
                    if k[g] < v:
                        k[g] = v
            if k[f] < d.idx:
                k[f] = d.idx

    def op(self, eng, fn, reads=(), writes=()):
        reads = [b for b in reads if b is not None]
        writes = [b for b in writes if b is not None]
        o = Op(eng, len(self.q[eng]), fn)
        self._deps(o, reads, writes)
        o.clock = dict(self.known[eng])
        o.clock[eng] = o.idx
        for b in reads:
            b.r.append(o)
        for b in writes:
            b.w = o
            b.r = []
        self.q[eng].append(o)
        return o

    def dma(self, eng, out_ap, in_ap, reads=(), writes=(), sbuf=None, **kw):
        reads = list(reads)
        writes = list(writes)
        cls = "sw" if eng == "pool" else "hw"
        if sbuf.dsem is None:
            sbuf.dsem = {}
            sbuf.dcount = {}
        if cls not in sbuf.dsem:
            sbuf.dsem[cls] = self.new_sem("d%d" % len(self.dsems))
            sbuf.dcount[cls] = 0
            self.dsems.append((sbuf, cls))
        sbuf.dcount[cls] += 16
        tok = (sbuf.dsem[cls], sbuf.dcount[cls])

        def fn(e, out_ap=out_ap, in_ap=in_ap, tok=tok, kw=kw):
            return e.dma_start(out=out_ap, in_=in_ap, **kw).then_inc(tok[0], 16)

        o = Op(eng, len(self.q[eng]), fn)
        o.trig = True
        self._deps(o, reads, writes)
        o.clock = dict(self.known[eng])
        self.q[eng].append(o)
        d = Op("dma", -1, None)
        d.is_dma = True
        d.dtok = tok
        for b in reads:
            b.r.append(d)
        for b in writes:
            b.w = d
            b.r = []
        return d

    def _all_dtoks(self):
        return [(b.dsem[c], b.dcount[c]) for b, c in self.dsems]

    def wait_dmas(self, eng="sp"):
        def fn(e):
            return e.nop()
        o = Op(eng, len(self.q[eng]), fn)
        for tok in self._all_dtoks():
            self._need_dtok(o, tok)
        o.clock = dict(self.known[eng])
        o.clock[eng] = o.idx
        self.q[eng].append(o)

    def barrier(self):
        marks = {}
        for e in ENGS:
            m = None
            for o in reversed(self.q[e]):
                if not o.trig:
                    m = o
                    break
            marks[e] = m
        for e in ENGS:
            def fn(en):
                return en.nop()
            o = Op(e, len(self.q[e]), fn)
            for f in ENGS:
                m = marks[f]
                if f != e and m is not None and self.known[e][f] < m.idx:
                    o.waits[f] = m
                    m.sig = True
                    self.known[e][f] = m.idx
            for tok in self._all_dtoks():
                self._need_dtok(o, tok)
            o.clock = dict(self.known[e])
            o.clock[e] = o.idx
            self.q[e].append(o)

    def emit(self):
        nc = self.nc
        esem = {e: self.new_sem("e_" + e) for e in ENGS}
        signum = {}
        for e in ENGS:
            c = 0
            for o in self.q[e]:
                if o.sig:
                    assert not o.trig
                    c += 1
                    signum[o] = c
        self.stats = {e: (len(self.q[e]), sum(1 for o in self.q[e] if o.sig),
                          sum(len(o.waits) + len(o.dwaits) for o in self.q[e])) for e in ENGS}

        def run(e, engobj):
            for o in self.q[e]:
                for f, d in o.waits.items():
                    engobj.wait_ge(esem[f], signum[d])
                for sem, val in o.dwaits:
                    engobj.wait_ge(sem, val)
                ins = o.fn(engobj)
                if o.sig:
                    ins.then_inc(esem[e], 1)

        with nc.Block() as block:
            @block.tensor
            def _(en):
                run("pe", en)

            @block.scalar
            def _(en):
                run("act", en)

            @block.vector
            def _(en):
                run("dve", en)

            @block.gpsimd
            def _(en):
                run("pool", en)

            @block.sync
            def _(en):
                run("sp", en)


D = 1024
SEQ = 2048
NTOK = SEQ + 128
DEPTH = 2
DFF = 2816
NF = 22
EPS = 1e-6
CP = 32
GP = 4
CS = 8
GS = 16
NTB = 256
TPB = NTB // 128
NPB = SEQ // NTB
C_QA, C_KA, C_VA, C_QR, C_FR, C_IR, C_GR, C_BC, C_CC, C_XC = 0, 384, 768, 1152, 1536, 1920, 2304, 2688, 2944, 3200


def _mult(delta):
    m = 0
    if 0 <= delta <= 128:
        m += 1
    if delta % 4 == 0 and 0 <= delta <= 512:
        m += 1
    if delta % 16 == 0 and 0 <= delta <= 2048:
        m += 1
    return m


def host_consts():
    c = {}
    c["ident"] = np.eye(128, dtype=np.float32)
    half = 32
    freqs = (10000.0 ** (-np.arange(half, dtype=np.float32) / half)).astype(np.float32)
    pos_p = np.arange(SEQ, dtype=np.float32)
    ang = pos_p[:, None] * freqs[None, :]
    c["cos_p"] = np.cos(ang).astype(np.float32)
    c["sin_p"] = np.sin(ang).astype(np.float32)
    pos_s = (2048 + np.arange(8, dtype=np.float32))
    ang_s = np.tile(pos_s[:, None] * freqs[None, :], (16, 1))
    c["cos_s"] = np.cos(ang_s).astype(np.float32)
    c["sin_s"] = np.sin(ang_s).astype(np.float32)
    kl = np.arange(128)[:, None]
    x = np.arange(19 * 128)[None, :]
    dl = x - 384 - kl
    mv = np.vectorize(_mult)
    c["mstrip"] = mv(dl).astype(np.float32)
    g = np.arange(128)[:, None, None]
    r = np.arange(16)[None, :, None]
    t = np.arange(8)[None, None, :]
    c["mc"] = mv(2048 + t - 16 * g - r).astype(np.float32).reshape(128, 128)
    bp = np.arange(128)[:, None] // 8
    tp = np.arange(128)[:, None] % 8
    bq = np.arange(128)[None, :] // 8
    tq = np.arange(128)[None, :] % 8
    c["mnew"] = (mv(tq - tp) * (bp == bq)).astype(np.float32)
    s = np.arange(128)[:, None]
    tt = np.arange(128)[None, :]
    for nm, C in (("P", CP), ("S", CS)):
        same = (s // C) == (tt // C)
        c["tri" + nm] = (same & (s <= tt)).astype(np.float32)
        c["u" + nm] = (same & (s > tt)).astype(np.float32)
    c["rowmP"] = (np.arange(128)[:, None] // CP == np.arange(GP)[None, :]).astype(np.float32)
    c["rowmS"] = (np.arange(128)[:, None] // CS == np.arange(GS)[None, :]).astype(np.float32)
    return c


CONST_SHAPES = {"ident": [128, 128], "cos_p": [SEQ, 32], "sin_p": [SEQ, 32], "cos_s": [128, 32], "sin_s": [128, 32],
                "mstrip": [128, 19 * 128], "mc": [128, 128], "mnew": [128, 128], "triP": [128, 128], "uP": [128, 128],
                "triS": [128, 128], "uS": [128, 128], "rowmP": [128, GP], "rowmS": [128, GS]}

IN_SHAPES = {
    "xp": [SEQ, D], "xs": [128, D], "call": [17, D],
    "ck": [DEPTH, 16, 2048, 384], "cv": [DEPTH, 16, 2048, 384],
    "sh": [DEPTH, 16, 6, 64, 64], "sc": [DEPTH, 16, 2, 256],
    "w_ada": [DEPTH, D, 6 * D], "b_ada": [DEPTH, 48, 128], "w_in": [DEPTH, D, 3456],
    "conv_w": [DEPTH, 3, 256], "conv_b": [DEPTH, 256], "lbl": [DEPTH, 384], "nw": [DEPTH, 384],
    "w_out": [DEPTH, D, D], "w_ffn_in": [DEPTH, D, 2 * DFF], "w_ffn_out": [DEPTH, DFF, D], "fnw": [8, 128],
}
OUT_SHAPES = {
    "yp": [SEQ, D], "ys": [128, D], "kp": [DEPTH, SEQ, 384], "vp": [DEPTH, SEQ, 384],
    "hp": [DEPTH, 6, 64, 64], "cvp": [DEPTH, 2, 256], "ks": [DEPTH, 128, 384], "vs": [DEPTH, 128, 384],
    "hs": [DEPTH, 16, 6, 64, 64], "cvs": [DEPTH, 16, 2, 256],
}


def sap(t, off, dims):
    return bass.AP(t, off, dims)


def build_program(stage=99):
    nc = bass.Bass("TRN2", target_bir_lowering=False)
    I = {n: nc.dram_tensor(n, s, F32, kind="ExternalInput").ap() for n, s in IN_SHAPES.items()}
    CI = {n: nc.dram_tensor("c_" + n, s, F32, kind="ExternalInput").ap() for n, s in CONST_SHAPES.items()}
    O = {n: nc.dram_tensor(n, s, F32, kind="ExternalOutput").ap() for n, s in OUT_SHAPES.items()}
    P = Prog(nc)
    es = ExitStack()

    def sb(name, shape, dt=F32):
        return es.enter_context(nc.sbuf_tensor(name, shape, dt))

    es0 = ExitStack()

    def sb0(name, shape, dt=F32):
        return es0.enter_context(nc.sbuf_tensor(name, shape, dt))

    xTb = sb("xTb", [128, 8, NTB])
    modT = sb("modT", [128, DEPTH, 48, 17])
    ident = sb("ident", [128, 128])
    identb = sb("identb", [128, 128], BF16)
    onesb = sb("onesb", [128, 128], BF16)
    cosp = sb("cosp", [128, 16, 32])
    sinp = sb("sinp", [128, 16, 32])
    coss = sb("coss", [128, 32])
    sins = sb("sins", [128, 32])
    mstrip = sb("mstrip", [128, 19 * 128], BF16)
    mc = sb("mc", [128, 128], BF16)
    mnew = sb("mnew", [128, 128], BF16)
    triP = sb("triP", [128, 128])
    uP = sb("uP", [128, 128])
    triS = sb("triS", [128, 128])
    uS = sb("uS", [128, 128])
    triPb = sb("triPb", [128, 128], BF16)
    triSb = sb("triSb", [128, 128], BF16)
    rowmP = sb("rowmP", [128, GP], BF16)
    rowmS = sb("rowmS", [128, GS], BF16)
    oml_tm = sb("oml_tm", [128, DEPTH, 384])
    oml_fm = sb("oml_fm", [128, DEPTH, 3])
    nw_tm = sb("nw_tm", [128, DEPTH, 384])
    cw_fm = sb("cw_fm", [128, DEPTH, 2, 3])
    cb_fm = sb("cb_fm", [128, DEPTH, 2])
    fnw_fm = sb("fnw_fm", [128, 8])
    epsb = sb("epsb", [128, 1])
    ps = es.enter_context(nc.psum_tensor("ps", [128, 8, 512], F32))
    PB = [Buf("psb%d" % i) for i in range(8)]

    B_xT = [Buf("xT%d" % k) for k in range(8)]
    B_const = Buf("const")
    B_mod = [Buf("mod0"), Buf("mod1")]

    def psb(i):
        return ps[:, i, :]

    def psbf(i):
        return ps[:, i, :].bitcast(BF16)

    def mm(out, lhsT, rhs, start, stop, reads, bank):
        P.op("pe", lambda e: e.matmul(out, lhsT, rhs, start=start, stop=stop, skip_group_check=True),
             reads=reads, writes=[bank])

    def tr(out, in_, idn, reads, bank):
        P.op("pe", lambda e: e.transpose(out, in_, idn), reads=reads, writes=[bank])

    def act(out, in_, func, reads, writes, scale=1.0, bias=None, accum=None):
        def fn(e):
            kw = {}
            if bias is not None:
                kw["bias"] = bias
            if accum is not None:
                kw["accum_out"] = accum
            return e.activation(out, in_, func, scale=scale, **kw)
        P.op("act", fn, reads=reads, writes=writes)

    def vop(fn, reads, writes, eng="dve"):
        P.op(eng, fn, reads=reads, writes=writes)

    NSLOT = 3
    SLOTN = 3072
    wslots = [sb("wslot%d" % i, [128, SLOTN], BF16) for i in range(NSLOT)]
    wbufs = [Buf("wslot%d" % i) for i in range(NSLOT)]
    wstate = {"n": 0}
    scr = {}
    B_scr = [{k: Buf("scr%d%s" % (l, k)) for k in ("tm", "fm", "wo", "fi", "fo")} for l in range(DEPTH)]

    def wslot_next():
        i = wstate["n"] % NSLOT
        wstate["n"] += 1
        return wslots[i], wbufs[i]

    def wreq(key, src_ap, shape, pieces=None):
        t, b = wslot_next()
        n = int(np.prod(shape[1:]))
        assert n <= SLOTN
        view = t[:, 0:n].rearrange("p (a b) -> p a b", a=shape[1])
        P.dma("pool", view, src_ap, writes=[b], sbuf=b)
        return view, b

    def wspec(kind, l, idx):
        if kind == "tm":
            c0 = [C_QA, C_KA, C_VA, C_FR, C_IR, C_GR][idx]
            return 3072, [(w_rows(I["w_in"][l], c0, 384), 0, [8, 384])]
        if kind == "fm":
            c0 = [C_QR, C_FR, C_BC, C_BC + 384][idx]
            return 3072, [(w_rows(I["w_in"][l], c0, 384), 0, [8, 384])]
        if kind == "wo":
            return 2048, [(w_rows(I["w_out"][l], idx * 256, 256), 0, [8, 256])]
        if kind == "fi":
            return 2048, [(w_rows(I["w_ffn_in"][l], idx * 128, 128), 0, [8, 128]),
                          (w_rows(I["w_ffn_in"][l], DFF + idx * 128, 128), 1024, [8, 128])]
        if kind == "fo":
            return NF * 128, [(I["w_ffn_out"][l][:, idx * 128:(idx + 1) * 128].rearrange("(f p) c -> p f c", p=128), 0, [NF, 128])]
        raise KeyError(kind)

    WKINDS = [("tm", 6), ("fm", 4), ("wo", 4), ("fi", NF), ("fo", 8)]

    def prepass():
        for l in range(DEPTH):
            for kind, cnt in WKINDS:
                for idx in range(cnt):
                    n, pieces = wspec(kind, l, idx)
                    d = nc.dram_tensor("scr_%s_%d_%d" % (kind, l, idx), [128, n], BF16).ap()
                    scr[(kind, l, idx)] = (d, n)
                    for src_ap, off, shp in pieces:
                        P.dma("pool", d[:, off:off + shp[0] * shp[1]].rearrange("p (a b) -> p a b", a=shp[0]), src_ap,
                              writes=[B_scr[l][kind]], sbuf=B_scr[l][kind])

    def wget(kind, l, idx):
        d, n = scr[(kind, l, idx)]
        t, b = wslot_next()
        P.dma("sp", t[:, 0:n], d[:, 0:n], reads=[B_scr[l][kind]], writes=[b], sbuf=b)
        return t, b

    def w_rows(wap, c0, n):
        return wap[:, c0:c0 + n].rearrange("(k p) n -> p k n", p=128)

    P.dma("sp", ident[:, :], CI["ident"], writes=[B_const], sbuf=B_const)
    P.dma("sp", cosp[:, :, :], CI["cos_p"].rearrange("(j p) f -> p j f", p=128), writes=[B_const], sbuf=B_const)
    P.dma("sp", sinp[:, :, :], CI["sin_p"].rearrange("(j p) f -> p j f", p=128), writes=[B_const], sbuf=B_const)
    P.dma("sp", coss[:, :], CI["cos_s"], writes=[B_const], sbuf=B_const)
    P.dma("sp", sins[:, :], CI["sin_s"], writes=[B_const], sbuf=B_const)
    for nm, t in (("triP", triP), ("uP", uP), ("triS", triS), ("uS", uS)):
        P.dma("sp", t[:, :], CI[nm], writes=[B_const], sbuf=B_const)
    B_c2 = Buf("const2")
    P.dma("pool", mstrip[:, :], CI["mstrip"], writes=[B_c2], sbuf=B_c2)
    P.dma("pool", mc[:, :], CI["mc"], writes=[B_c2], sbuf=B_c2)
    P.dma("pool", mnew[:, :], CI["mnew"], writes=[B_c2], sbuf=B_c2)
    P.dma("pool", triPb[:, :], CI["triP"], writes=[B_c2], sbuf=B_c2)
    P.dma("pool", triSb[:, :], CI["triS"], writes=[B_c2], sbuf=B_c2)
    P.dma("pool", rowmP[:, :], CI["rowmP"], writes=[B_c2], sbuf=B_c2)
    P.dma("pool", rowmS[:, :], CI["rowmS"], writes=[B_c2], sbuf=B_c2)
    P.dma("pool", identb[:, :], CI["ident"], writes=[B_c2], sbuf=B_c2)
    prepass()
    B_c3 = Buf("const3")
    for l in range(DEPTH):
        P.dma("sp", nw_tm[:, l, :], I["nw"][l:l + 1, :].broadcast_to([128, 384]), writes=[B_c3], sbuf=B_c3)
        P.dma("sp", oml_fm[:, l, :], I["lbl"][l].rearrange("(m p) -> p m", p=128), writes=[B_c3], sbuf=B_c3,
              allow_slow_non_contiguous=True)
        for m in range(2):
            P.dma("sp", cw_fm[:, l, m, :], I["conv_w"][l][:, m * 128:(m + 1) * 128].rearrange("t p -> p t"), writes=[B_c3], sbuf=B_c3,
                  allow_slow_non_contiguous=True)
        P.dma("sp", cb_fm[:, l, :], I["conv_b"][l].rearrange("(m p) -> p m", p=128), writes=[B_c3], sbuf=B_c3,
              allow_slow_non_contiguous=True)
    P.dma("sp", fnw_fm[:, :], I["fnw"].rearrange("k p -> p k"), writes=[B_c3], sbuf=B_c3, allow_slow_non_contiguous=True)
    lbb = sb0("lbb", [128, 2, 384])
    for l in range(DEPTH):
        P.dma("sp", lbb[:, l, :], I["lbl"][l:l + 1, :].broadcast_to([128, 384]), writes=[B_c3], sbuf=B_c3)
    vop(lambda e: e.memset(onesb[:, :], 1.0), [], [B_const], eng="pool")
    vop(lambda e: e.memset(epsb[:, :], EPS), [], [B_const], eng="pool")
    vop(lambda e: e.memset(oml_tm[:, 0, :], 1.0), [], [B_c3], eng="pool")
    lbt = sb0("lbt", [128, 384])
    B_lbt = Buf("lbt")
    vop(lambda e: e.tensor_sub(lbt[:, :], lbb[:, 0, :], lbb[:, 1, :]), [B_c3], [B_lbt])
    act(lbt[:, :], lbt[:, :], AF.Exp, [B_lbt], [B_lbt])
    vop(lambda e: e.tensor_scalar_add(lbb[:, 0, :], lbt[:, :], 1.0), [B_lbt], [B_c3])
    vop(lambda e: e.reciprocal(lbb[:, 0, :], lbb[:, 0, :]), [B_c3], [B_c3])
    vop(lambda e: e.tensor_mul(oml_tm[:, 1, :], lbt[:, :], lbb[:, 0, :]), [B_c3, B_lbt], [B_c3])
    lbf = sb0("lbf", [128, 4, 3])
    vop(lambda e: e.tensor_sub(lbf[:, 0, :], oml_fm[:, 0, :], oml_fm[:, 1, :]), [B_c3], [B_lbt])
    act(lbf[:, 0, :], lbf[:, 0, :], AF.Exp, [B_lbt], [B_lbt])
    vop(lambda e: e.tensor_scalar_add(lbf[:, 1, :], lbf[:, 0, :], 1.0), [B_lbt], [B_lbt])
    vop(lambda e: e.reciprocal(lbf[:, 1, :], lbf[:, 1, :]), [B_lbt], [B_lbt])
    vop(lambda e: e.tensor_mul(oml_fm[:, 1, :], lbf[:, 0, :], lbf[:, 1, :]), [B_lbt, B_c3], [B_c3])
    vop(lambda e: e.memset(oml_fm[:, 0, :], 1.0), [B_c3], [B_c3])

    cin = sb0("cin", [17, D])
    scT = sb0("scT", [128, 8, 17], BF16)
    adab = [sb0("adab%d" % i, [128, 8, 128], BF16) for i in range(2)]
    B_adab = [Buf("adab0"), Buf("adab1")]
    ctmp = sb0("ctmp", [128, 3, 8 * 17])
    badaT = sb0("badaT", [128, DEPTH, 48])
    bin_ = sb0("bin_", [48, DEPTH, 128])
    B_cin, B_scT, B_ctmp, B_bada = Buf("cin"), Buf("scT"), Buf("ctmp"), Buf("bada")
    P.dma("sp", cin[:, :], I["call"], writes=[B_cin], sbuf=B_cin)
    P.dma("sp", bin_[:, :, :], I["b_ada"].rearrange("l m p -> m l p"), writes=[B_bada], sbuf=B_bada)
    for k in range(8):
        tr(ps[:, 0, k * 17:(k + 1) * 17], cin[:, k * 128:(k + 1) * 128], ident[0:17, 0:17], [B_cin, B_const], PB[0])
    act(ctmp[:, 0, :], ps[:, 0, 0:136], AF.Exp, [PB[0]], [B_ctmp, PB[0]], scale=-1.0)
    vop(lambda e: e.tensor_scalar_add(ctmp[:, 0, :], ctmp[:, 0, :], 1.0), [B_ctmp], [B_ctmp])
    vop(lambda e: e.reciprocal(ctmp[:, 0, :], ctmp[:, 0, :]), [B_ctmp], [B_ctmp])
    vop(lambda e: e.tensor_tensor(scT[:, :, :].rearrange("p k c -> p (k c)"), ps[:, 0, 0:136], ctmp[:, 0, :], ALU.mult),
        [B_ctmp, PB[0]], [B_scT, PB[0]])
    for l in range(DEPTH):
        tr(ps[:, 1, l * 48:(l + 1) * 48], bin_[:, l, :], ident[0:48, 0:48], [B_bada, B_const], PB[1])
    vop(lambda e: e.tensor_copy(badaT[:, :, :].rearrange("p l m -> p (l m)"), ps[:, 1, 0:96]), [PB[1]], [B_bada, PB[1]])
    for l in range(DEPTH):
        for c4 in range(12):
            bk = 2 + (c4 % 2)
            for m4 in range(4):
                c = c4 * 4 + m4
                t_, wb = wslot_next()
                wt32 = t_[:, 0:2048].bitcast(F32).rearrange("p (a b) -> p a b", a=8)
                P.dma("sp", wt32, w_rows(I["w_ada"][l], c * 128, 128), writes=[wb], sbuf=wb)
                ai = c % 2
                vop(lambda e, ai=ai, wt32=wt32: e.tensor_copy(adab[ai][:, :, :], wt32), [wb], [B_adab[ai]])
                for k in range(8):
                    mm(ps[:, bk, m4 * 17:(m4 + 1) * 17], adab[ai][:, k, :], scT[:, k, :],
                       (m4 == 0 and k == 0), (k == 7), [B_adab[ai], B_scT], PB[bk])
            vop(lambda e, l=l, c4=c4, bk=bk: e.tensor_tensor(
                modT[:, l, 4 * c4:4 * c4 + 4, :], ps[:, bk, 0:68].rearrange("p (m c) -> p m c", m=4),
                sap(badaT, l * 48 + 4 * c4, [[DEPTH * 48, 128], [1, 4], [0, 17]]), ALU.add),
                [PB[bk], B_bada], [B_mod[l], PB[bk]])
        vop(lambda e, l=l: e.tensor_scalar_add(modT[:, l, 8:16, :], modT[:, l, 8:16, :], 1.0), [B_mod[l]], [B_mod[l]])
        vop(lambda e, l=l: e.tensor_scalar_add(modT[:, l, 32:40, :], modT[:, l, 32:40, :], 1.0), [B_mod[l]], [B_mod[l]])


    P.barrier()
    es0.close()
    xst = [sb("xst%d" % i, [128, D]) for i in range(2)]
    B_xst = [Buf("xst0"), Buf("xst1")]

    def load_x(blkinfo):
        t0, nt, kind, bi = blkinfo
        s = 0
        for j in range(nt // 128):
            src = I["xp"][t0 + j * 128:t0 + (j + 1) * 128, :] if kind == "p" else I["xs"]
            P.dma("sp", xst[s][:, :], src, writes=[B_xst[s]], sbuf=B_xst[s])
            for half in range(2):
                bk = 4 + half
                for kk in range(4):
                    k = half * 4 + kk
                    tr(ps[:, bk, kk * 128:(kk + 1) * 128], xst[s][:, k * 128:(k + 1) * 128], ident[:, :], [B_xst[s], B_const], PB[bk])
                dst = xTb[:, half * 4:half * 4 + 4, j * 128:(j + 1) * 128]
                src_ps = ps[:, bk, :].rearrange("p (k t) -> p k t", k=4)
                wr = [B_xT[k] for k in range(half * 4, half * 4 + 4)] + [PB[bk]]
                if half == 0:
                    act(dst, src_ps, AF.Copy, [PB[bk]], wr)
                else:
                    vop(lambda e, dst=dst, src_ps=src_ps: e.tensor_copy(dst, src_ps), [PB[bk]], wr)

    hm = sb("hm", [128, 8, NTB], BF16)
    B_hm = [Buf("hm%d" % k) for k in range(8)]
    sq = [sb("sq%d" % i, [128, NTB], BF16) for i in range(2)]
    B_sq = [Buf("sq0"), Buf("sq1")]
    rstd = sb("rstd", [128, NTB])
    B_rstd = Buf("rstd")
    ntmp = [sb("ntmp%d" % i, [128, 256]) for i in range(2)]
    B_ntmp = [Buf("ntmp0"), Buf("ntmp1")]

    BLOCKS = [(i * NTB, NTB, "p", i) for i in range(NPB)] + [(2048, 128, "s", NPB)]

    def mod_bc(l, chunk):
        off = (l * 48 + chunk) * 17 + 1
        return sap(modT, off, [[DEPTH * 48 * 17, 128], [1, 16], [0, 8]])

    def ssq_rstd(nt):
        for k in range(8):
            s = k % 2
            act(sq[s][:, 0:nt], xTb[:, k, 0:nt], AF.Square, [B_xT[k]], [B_sq[s]])
            mm(ps[:, 0, 0:nt], onesb[:, :], sq[s][:, 0:nt], k == 0, k == 7, [B_sq[s], B_const], PB[0])
        act(rstd[:, 0:nt], ps[:, 0, 0:nt], AF.Ln, [PB[0], B_const], [B_rstd, PB[0]], scale=1.0 / D, bias=epsb[:, 0:1])
        act(rstd[:, 0:nt], rstd[:, 0:nt], AF.Exp, [B_rstd], [B_rstd], scale=-0.5)

    def norm_block(l, blkinfo, which):
        t0, nt, kind, bi = blkinfo
        sh0 = 0 if which == 0 else 24
        sc0 = 8 if which == 0 else 32
        ssq_rstd(nt)
        for k in range(8):
            s = k % 2
            vop(lambda e, k=k, s=s: e.tensor_mul(ntmp[s][:, 0:nt], xTb[:, k, 0:nt], rstd[:, 0:nt]),
                [B_xT[k], B_rstd], [B_ntmp[s]])
            if kind == "p":
                act(hm[:, k, 0:nt], ntmp[s][:, 0:nt], AF.Identity, [B_ntmp[s], B_mod[l]], [B_hm[k]],
                    scale=modT[:, l, sc0 + k, 0:1], bias=modT[:, l, sh0 + k, 0:1])
            else:
                v3 = ntmp[s][:, 0:128].rearrange("p (b t) -> p b t", b=16)
                vop(lambda e, v3=v3, k=k: e.tensor_mul(v3, v3, mod_bc(l, sc0 + k)), [B_ntmp[s], B_mod[l]], [B_ntmp[s]])
                vop(lambda e, v3=v3, k=k: e.tensor_add(hm[:, k, 0:128].rearrange("p (b t) -> p b t", b=16), v3, mod_bc(l, sh0 + k)),
                    [B_ntmp[s], B_mod[l]], [B_hm[k]])

    def resid_update(l, blkinfo, m, bank, gchunk0):
        t0, nt, kind, bi = blkinfo
        if kind == "p":
            vop(lambda e: e.scalar_tensor_tensor(xTb[:, m, 0:nt], ps[:, bank, 0:nt], modT[:, l, gchunk0 + m, 0:1],
                                                 xTb[:, m, 0:nt], op0=ALU.mult, op1=ALU.add),
                [PB[bank], B_mod[l], B_xT[m]], [B_xT[m], PB[bank]])
        else:
            s = m % 2
            v3 = ntmp[s][:, 0:128].rearrange("p (b t) -> p b t", b=16)
            vop(lambda e: e.tensor_tensor(v3, ps[:, bank, 0:128].rearrange("p (b t) -> p b t", b=16), mod_bc(l, gchunk0 + m), ALU.mult),
                [PB[bank], B_mod[l]], [B_ntmp[s], PB[bank]])
            vop(lambda e: e.tensor_add(xTb[:, m, 0:128], xTb[:, m, 0:128], ntmp[s][:, 0:128]),
                [B_ntmp[s], B_xT[m]], [B_xT[m]])

    KT = sb("KT", [128, DEPTH, 3, SEQ], BF16)
    Vb = sb("Vb", [128, DEPTH, 16 * 384], BF16)
    B_KT = [[Buf("KT%d_%d" % (l, i)) for i in range(16)] for l in range(DEPTH)]
    B_V = [[Buf("V%d_%d" % (l, i)) for i in range(16)] for l in range(DEPTH)]
    QT = sb("QT", [128, 3, NTB], BF16)
    B_QT = [Buf("QT%d" % j) for j in range(TPB)]
    stg = [sb("stg%d" % i, [128, 384]) for i in range(2)]
    B_stg = [Buf("stg%d" % i) for i in range(2)]
    rp = [sb("rp%d" % i, [128, 384]) for i in range(3)]
    B_rp = [Buf("rp%d" % i) for i in range(3)]
    tb16 = [sb("tb16_%d" % i, [128, 384], BF16) for i in range(2)]
    B_tb16 = [Buf("tb16_0"), Buf("tb16_1")]
    stg_ctr = {"n": 0, "t": 0}

    def rope(src_ps, bank, dst, j, kind, dstbuf):
        if kind == "p":
            cos2 = sap(cosp, j * 32, [[512, 128], [0, 6], [0, 2], [1, 32]])
            sin1 = sap(sinp, j * 32, [[512, 128], [0, 6], [1, 32]])
        else:
            cos2 = sap(coss, 0, [[32, 128], [0, 6], [0, 2], [1, 32]])
            sin1 = sap(sins, 0, [[32, 128], [0, 6], [1, 32]])
        x4 = src_ps.rearrange("p (h two f) -> p h two f", h=6, two=2)
        a4 = rp[0][:, :].rearrange("p (h two f) -> p h two f", h=6, two=2)
        b4 = rp[1][:, :].rearrange("p (h two f) -> p h two f", h=6, two=2)
        vop(lambda e: e.tensor_tensor(a4, x4, cos2, ALU.mult), [PB[bank], B_const], [B_rp[0], PB[bank]])
        vop(lambda e: e.tensor_tensor(b4[:, :, 0, :], x4[:, :, 1, :], sin1, ALU.mult), [PB[bank], B_const], [B_rp[1], PB[bank]])
        vop(lambda e: e.tensor_tensor(b4[:, :, 1, :], x4[:, :, 0, :], sin1, ALU.mult), [PB[bank], B_const], [B_rp[1], PB[bank]])
        d4 = dst.rearrange("p (h two f) -> p h two f", h=6, two=2)
        vop(lambda e: e.tensor_sub(d4[:, :, 0, :], a4[:, :, 0, :], b4[:, :, 0, :]), [B_rp[0], B_rp[1]], [dstbuf])
        vop(lambda e: e.tensor_add(d4[:, :, 1, :], a4[:, :, 1, :], b4[:, :, 1, :]), [B_rp[0], B_rp[1]], [dstbuf])

    def tokmajor_group(l, blkinfo, col0, key, consumer):
        t0, nt, kind, bi = blkinfo
        wt_, wb = wget("tm", l, [C_QA, C_KA, C_VA, C_FR, C_IR, C_GR].index(col0))
        wt = wt_[:, 0:3072].rearrange("p (a b) -> p a b", a=8)
        for j in range(nt // 128):
            bk = 1 + (stg_ctr["t"] % 2)
            stg_ctr["t"] += 1
            for k in range(8):
                mm(ps[:, bk, 0:384], hm[:, k, j * 128:(j + 1) * 128], wt[:, k, :], k == 0, k == 7, [B_hm[k], wb], PB[bk])
            consumer(j, bk)

    def cons_q(l, blkinfo):
        t0, nt, kind, bi = blkinfo

        def f(j, bk):
            rope(ps[:, bk, 0:384], bk, rp[2][:, :], (t0 // 128 + j), kind, B_rp[2])
            i16 = j % 2
            vop(lambda e: e.tensor_copy(tb16[i16][:, :], rp[2][:, :]), [B_rp[2]], [B_tb16[i16]])
            for hp in range(3):
                tr(psbf(3)[:, hp * 128:(hp + 1) * 128], tb16[i16][:, hp * 128:(hp + 1) * 128], identb[:, :], [B_tb16[i16], B_c2], PB[3])
            act(QT[:, :, j * 128:(j + 1) * 128], psbf(3)[:, 0:384].rearrange("p (h t) -> p h t", h=3), AF.Copy,
                [PB[3]], [B_QT[j], PB[3]])
        return f

    def cons_k(l, blkinfo, KTview, B_KTt, out_ap):
        t0, nt, kind, bi = blkinfo

        def f(j, bk):
            si = stg_ctr["n"] % 2
            stg_ctr["n"] += 1
            rope(ps[:, bk, 0:384], bk, stg[si][:, :], (t0 // 128 + j), kind, B_stg[si])
            row0 = (t0 + j * 128) if kind == "p" else 0
            P.dma("pool", out_ap[row0:row0 + 128, :], stg[si][:, :], reads=[B_stg[si]], sbuf=B_stg[si])
            i16 = j % 2
            vop(lambda e: e.tensor_copy(tb16[i16][:, :], stg[si][:, :]), [B_stg[si]], [B_tb16[i16]])
            for hp in range(3):
                tr(psbf(3)[:, hp * 128:(hp + 1) * 128], tb16[i16][:, hp * 128:(hp + 1) * 128], identb[:, :], [B_tb16[i16], B_c2], PB[3])
            tj = (t0 // 128 + j) if kind == "p" else 0
            act(KTview(tj), psbf(3)[:, 0:384].rearrange("p (h t) -> p h t", h=3), AF.Copy,
                [PB[3]], [B_KTt[tj], PB[3]])
        return f

    def cons_v(l, blkinfo, Vt_fn, B_Vt, out_ap):
        t0, nt, kind, bi = blkinfo

        def f(j, bk):
            si = stg_ctr["n"] % 2
            stg_ctr["n"] += 1
            act(stg[si][:, :], ps[:, bk, 0:384], AF.Copy, [PB[bk]], [B_stg[si], PB[bk]])
            row0 = (t0 + j * 128) if kind == "p" else 0
            P.dma("pool", out_ap[row0:row0 + 128, :], stg[si][:, :], reads=[B_stg[si]], sbuf=B_stg[si])
            tj = (t0 // 128 + j) if kind == "p" else 0
            vop(lambda e: e.tensor_copy(Vt_fn(tj), stg[si][:, :]), [B_stg[si]], [B_Vt[tj]])
        return f

    Pb = [sb("Pb%d" % i, [128, NTB], BF16) for i in range(3)]
    B_Pb = [Buf("Pb%d" % i) for i in range(3)]
    rden = sb("rden", [128, NTB])
    B_rden = Buf("rden")
    pctr = {"n": 0}

    def attn_prompt(l, blk):
        assert 2 * NTB <= 512
        ob = 0
        for h in range(6):
            hp, pb = h // 2, 64 * (h % 2)
            ktmax = TPB * blk + TPB - 1
            for kt in range(ktmax + 1):
                q0 = max(0, kt - TPB * blk) * 128
                ncol = NTB - q0
                sbk = 1 + (pctr["n"] % 2)
                pi = pctr["n"] % 3
                pctr["n"] += 1
                jr = [B_QT[j] for j in range(q0 // 128, TPB)]
                mm(ps[:, sbk, q0:NTB], KT[pb:pb + 64, l, hp, kt * 128:(kt + 1) * 128], QT[pb:pb + 64, hp, q0:NTB], True, True,
                   [B_KT[l][kt]] + jr, PB[sbk])
                act(Pb[pi][:, q0:NTB], ps[:, sbk, q0:NTB], AF.Exp, [PB[sbk]], [B_Pb[pi], PB[sbk]], scale=0.125)
                c0 = 128 * (TPB * blk + q0 // 128 - kt + 3)
                vop(lambda e, pi=pi, q0=q0, c0=c0, ncol=ncol: e.tensor_mul(Pb[pi][:, q0:NTB], Pb[pi][:, q0:NTB], mstrip[:, c0:c0 + ncol]),
                    [B_Pb[pi], B_c2], [B_Pb[pi]])
                voff = kt * 384 + h * 64
                mm(ps[0:64, ob, q0:NTB], Vb[:, l, voff:voff + 64], Pb[pi][:, q0:NTB], kt == 0, kt == ktmax, [B_V[l][kt], B_Pb[pi]], PB[ob])
                mm(ps[0:64, ob, NTB + q0:2 * NTB], onesb[:, 0:64], Pb[pi][:, q0:NTB], False, kt == ktmax, [B_const, B_Pb[pi]], PB[ob])
                yield
            vop(lambda e: e.reciprocal(rden[0:64, :], ps[0:64, ob, NTB:2 * NTB]), [PB[ob]], [B_rden, PB[ob]])
            vop(lambda e, hp=hp, pb=pb: e.tensor_mul(hm[pb:pb + 64, hp, 0:NTB], ps[0:64, ob, 0:NTB], rden[0:64, :]),
                [PB[ob], B_rden], [B_hm[hp], PB[ob]])
            yield

    k_tm = sb("k_tm", [128, TPB, 384], BF16)
    lf_tm = sb("lf_tm", [128, TPB, 384])
    Vh = sb("Vh", [128, TPB, 384], BF16)
    gate = sb("gate", [128, TPB, 384], BF16)
    B_ktm = [Buf("ktm%d" % j) for j in range(TPB)]
    B_lf = [Buf("lf%d" % j) for j in range(TPB)]
    B_Vh = [Buf("Vh%d" % j) for j in range(TPB)]
    B_gate = [Buf("gate%d" % j) for j in range(TPB)]
    qTh = sb("qTh", [64, 6, NTB], BF16)
    kTh = sb("kTh", [64, 6, NTB], BF16)
    B_qTh, B_kTh = Buf("qTh"), Buf("kTh")
    bcT = sb("bcT", [128, 2, NTB], BF16)
    ccT = sb("ccT", [128, 2, NTB])
    uT = sb("uT", [128, 2, max(NTB + 2, 160)])
    uTc = sb("uTc", [128, DEPTH, 2, 2])
    B_bcT = [Buf("bc0"), Buf("bc1")]
    B_ccT = [Buf("cc0"), Buf("cc1")]
    B_uT = [Buf("u0"), Buf("u1")]
    B_uTc = [Buf("uc0"), Buf("uc1")]
    et = [sb("et%d" % i, [128, 384]) for i in range(2)]
    B_et = [Buf("et0"), Buf("et1")]
    onesf = sb("onesf", [128, 1])
    vop(lambda e: e.memset(onesf[:, :], 1.0), [], [B_const], eng="pool")
    vop(lambda e: e.memset(uTc[:, :, :, :], 0.0), [], [B_uTc[0], B_uTc[1]], eng="pool")

    def cons_fr(l, blkinfo):
        def f(j, bk):
            e0 = et[0][:, 0:384]
            act(e0, ps[:, bk, 0:384], AF.Exp, [PB[bk]], [B_et[0], PB[bk]])
            vop(lambda e: e.tensor_scalar_add(e0, e0, 1.0), [B_et[0]], [B_et[0]])
            vop(lambda e: e.reciprocal(e0, e0), [B_et[0]], [B_et[0]])
            vop(lambda e: e.tensor_mul(e0, e0, oml_tm[:, l, :]), [B_et[0], B_c3], [B_et[0]])
            vop(lambda e: e.tensor_copy(k_tm[:, j, :], e0), [B_et[0]], [B_ktm[j]])
            act(lf_tm[:, j, :], e0, AF.Ln, [B_et[0], B_const], [B_lf[j]], scale=-1.0, bias=onesf[:, 0:1])
        return f

    def cons_ir(l, blkinfo):
        def f(j, bk):
            act(Vh[:, j, :], ps[:, bk, 0:384], AF.Copy, [PB[bk]], [B_Vh[j], PB[bk]])
        return f

    def cons_gr(l, blkinfo):
        def f(j, bk):
            e1 = et[1][:, 0:384]
            act(e1, ps[:, bk, 0:384], AF.Exp, [PB[bk]], [B_et[1], PB[bk]], scale=-1.0)
            vop(lambda e: e.tensor_scalar_add(e1, e1, 1.0), [B_et[1]], [B_et[1]])
            vop(lambda e: e.reciprocal(e1, e1), [B_et[1]], [B_et[1]])
            vop(lambda e: e.tensor_mul(e1, e1, nw_tm[:, l, :]), [B_et[1], B_c3], [B_et[1]])
            vop(lambda e: e.tensor_tensor(gate[:, j, :], ps[:, bk, 0:384], e1, ALU.mult), [B_et[1], PB[bk]], [B_gate[j], PB[bk]])
        return f

    def featmajor_groups(l, blkinfo):
        t0, nt, kind, bi = blkinfo
        groups = [("qr", C_QR), ("fr", C_FR), ("cA", C_BC), ("cB", C_BC + 384)]
        for gi, (key, col0) in enumerate(groups):
            wt_, wb = wget("fm", l, gi)
            wt = wt_[:, 0:3072].rearrange("p (a b) -> p a b", a=8)
            for m3 in range(3):
                bk = 1 + (stg_ctr["t"] % 2)
                stg_ctr["t"] += 1
                for k in range(8):
                    mm(ps[:, bk, 0:nt], wt[:, k, m3 * 128:(m3 + 1) * 128], hm[:, k, 0:nt], k == 0, k == 7, [B_hm[k], wb], PB[bk])
                pv = ps[:, bk, 0:nt]
                if key == "qr":
                    for hf in range(2):
                        act(qTh[:, 2 * m3 + hf, 0:nt], ps[64 * hf:64 * hf + 64, bk, 0:nt], AF.Copy, [PB[bk]], [B_qTh, PB[bk]])
                elif key == "fr":
                    e0 = et[0][:, 0:nt]
                    act(e0, pv, AF.Exp, [PB[bk]], [B_et[0], PB[bk]])
                    vop(lambda e, e0=e0: e.tensor_scalar_add(e0, e0, 1.0), [B_et[0]], [B_et[0]])
                    vop(lambda e, e0=e0: e.reciprocal(e0, e0), [B_et[0]], [B_et[0]])
                    vop(lambda e, e0=e0, m3=m3: e.tensor_scalar_mul(e0, e0, oml_fm[:, l, m3:m3 + 1]), [B_et[0], B_c3], [B_et[0]])
                    for hf in range(2):
                        act(kTh[:, 2 * m3 + hf, 0:nt], et[0][64 * hf:64 * hf + 64, 0:nt], AF.Copy, [B_et[0]], [B_kTh])
                else:
                    ci = (0 if key == "cA" else 3) + m3
                    if ci < 2:
                        act(bcT[:, ci, 0:nt], pv, AF.Copy, [PB[bk]], [B_bcT[ci], PB[bk]])
                    elif ci < 4:
                        act(ccT[:, ci - 2, 0:nt], pv, AF.Copy, [PB[bk]], [B_ccT[ci - 2], PB[bk]])
                    else:
                        mi = ci - 4
                        if kind == "p":
                            dst = uT[:, mi, 2:2 + nt]
                            vop(lambda e, dst=dst, pv=pv, mi=mi: e.tensor_tensor(dst, pv, ccT[:, mi, 0:nt], ALU.mult),
                                [PB[bk], B_ccT[mi]], [B_uT[mi], PB[bk]])
                        else:
                            dst = uT[:, mi, 0:160].rearrange("p (b t) -> p b t", b=16)[:, :, 2:10]
                            vop(lambda e, dst=dst, pv=pv, mi=mi: e.tensor_tensor(
                                dst, pv.rearrange("p (b t) -> p b t", b=16), ccT[:, mi, 0:128].rearrange("p (b t) -> p b t", b=16), ALU.mult),
                                [PB[bk], B_ccT[mi]], [B_uT[mi], PB[bk]])

    cacc = [sb("cacc%d" % i, [128, NTB]) for i in range(2)]
    B_cacc = [Buf("cacc0"), Buf("cacc1")]

    def conv_pre(l, blkinfo):
        t0, nt, kind, bi = blkinfo
        for mi in range(2):
            if kind == "p":
                vop(lambda e, mi=mi: e.tensor_copy(uT[:, mi, 0:2], uTc[:, l, mi, :]), [B_uTc[l]], [B_uT[mi]])
            else:
                for b in range(16):
                    P.dma("sp", uT[:, mi, b * 10:b * 10 + 2], I["sc"][l][b, :, mi * 128:(mi + 1) * 128].rearrange("j p -> p j"),
                          writes=[B_uT[mi]], sbuf=B_uT[mi], allow_slow_non_contiguous=True)

    def conv_block(l, blkinfo):
        t0, nt, kind, bi = blkinfo
        for mi in range(2):
            a = cacc[mi]
            if kind == "p":
                def uv(sh, mi=mi):
                    return uT[:, mi, sh:sh + nt]
                av = a[:, 0:nt]
                bcv = bcT[:, mi, 0:nt]
                ov = hm[:, 6 + mi, 0:nt]
            else:
                def uv(sh, mi=mi):
                    return uT[:, mi, 0:160].rearrange("p (b t) -> p b t", b=16)[:, :, sh:sh + 8]
                av = a[:, 0:128].rearrange("p (b t) -> p b t", b=16)
                bcv = bcT[:, mi, 0:128].rearrange("p (b t) -> p b t", b=16)
                ov = hm[:, 6 + mi, 0:128].rearrange("p (b t) -> p b t", b=16)
            w = lambda tap, mi=mi: cw_fm[:, l, mi, tap:tap + 1]
            vop(lambda e, uv=uv, av=av, w=w, mi=mi: e.tensor_scalar(av, uv(2), w(2), cb_fm[:, l, mi:mi + 1], op0=ALU.mult, op1=ALU.add),
                [B_uT[mi], B_c3], [B_cacc[mi]])
            vop(lambda e, uv=uv, av=av, w=w: e.scalar_tensor_tensor(av, uv(1), w(1), av, op0=ALU.mult, op1=ALU.add),
                [B_uT[mi], B_c3, B_cacc[mi]], [B_cacc[mi]])
            vop(lambda e, uv=uv, av=av, w=w: e.scalar_tensor_tensor(av, uv(0), w(0), av, op0=ALU.mult, op1=ALU.add),
                [B_uT[mi], B_c3, B_cacc[mi]], [B_cacc[mi]])
            vop(lambda e, av=av, bcv=bcv, ov=ov: e.tensor_mul(ov, av, bcv), [B_cacc[mi], B_bcT[mi]], [B_hm[6 + mi]])
            if kind == "p":
                vop(lambda e, mi=mi: e.tensor_copy(uTc[:, l, mi, :], uT[:, mi, nt:nt + 2]), [B_uT[mi]], [B_uTc[l]])
                if bi == NPB - 1:
                    P.dma("sp", O["cvp"][l][:, mi * 128:(mi + 1) * 128].rearrange("j p -> p j"), uT[:, mi, nt:nt + 2],
                          reads=[B_uT[mi]], sbuf=B_uT[mi], allow_slow_non_contiguous=True)
            else:
                for b in range(16):
                    P.dma("sp", O["cvs"][l][b, :, mi * 128:(mi + 1) * 128].rearrange("j p -> p j"),
                          uT[:, mi, b * 10 + 8:b * 10 + 10], reads=[B_uT[mi]], sbuf=B_uT[mi], allow_slow_non_contiguous=True)

    Sst = sb("Sst", [64, DEPTH, 6, 64])
    Sbf = sb("Sbf", [64, DEPTH, 6, 64], BF16)
    B_S = [Buf("S0"), Buf("S1")]
    B_Sbf = [Buf("Sbf0"), Buf("Sbf1")]
    eb = sb("eb", [64, 6, 128])
    enb = sb("enb", [64, 6, 128])
    B_eb, B_enb = Buf("eb"), Buf("enb")
    QZN = 6 * GP * 128
    Qz = sb("Qz", [64, QZN], BF16)
    B_Qz = Buf("Qz")
    KtT = sb("KtT", [64, 6, 128], BF16)
    B_KtT = Buf("KtT")
    Khat = sb("Khat", [128, 384], BF16)
    B_Khat = Buf("Khat")
    Khz = sb("Khz", [128, 4, 384], BF16)
    B_Khz = Buf("Khz")
    Am = sb("Am", [128, 6, 128], BF16)
    B_Am = Buf("Am")
    osq = sb("osq", [128, 384])
    B_osq = Buf("osq")
    oss = sb("oss", [128, 8])
    B_oss = Buf("oss")
    onb = sb("onb", [128, 384], BF16)
    B_onb = Buf("onb")
    Vz = sb("Vz", [128, 16, 64], BF16)
    B_Vz = Buf("Vz")
    S0 = sb("S0", [64, 16, 64])
    S0b = sb("S0b", [64, 16, 64], BF16)
    B_S0, B_S0b = Buf("S0s"), Buf("S0bs")
    vop(lambda e: e.memset(Sst[:, :, :, :], 0.0), [], [B_S[0], B_S[1]], eng="pool")
    vop(lambda e: e.memset(Sbf[:, :, :, :], 0.0), [], [B_Sbf[0], B_Sbf[1]], eng="pool")
    vop(lambda e: e.memset(Qz[:, :], 0.0), [], [B_Qz], eng="pool")

    def hgrn_tile(l, blkinfo, j):
        t0, nt, kind, bi = blkinfo
        C, G = (CP, GP) if kind == "p" else (CS, GS)
        tri, uu, trib = (triP, uP, triPb) if kind == "p" else (triS, uS, triSb)
        tc0 = j * 128
        for h in range(6):
            bk = 4 if h < 4 else 5
            mm(ps[0:64, bk, (h % 4) * 128:(h % 4 + 1) * 128], lf_tm[:, j, h * 64:(h + 1) * 64], tri[:, :], True, True,
               [B_lf[j], B_const], PB[bk])
        act(eb[:, 0:4, :], ps[0:64, 4, :].rearrange("p (h t) -> p h t", h=4), AF.Exp, [PB[4]], [B_eb, PB[4]])
        act(enb[:, 0:4, :], ps[0:64, 4, :].rearrange("p (h t) -> p h t", h=4), AF.Exp, [PB[4]], [B_enb, PB[4]], scale=-1.0)
        act(eb[:, 4:6, :], ps[0:64, 5, 0:256].rearrange("p (h t) -> p h t", h=2), AF.Exp, [PB[5]], [B_eb, PB[5]])
        act(enb[:, 4:6, :], ps[0:64, 5, 0:256].rearrange("p (h t) -> p h t", h=2), AF.Exp, [PB[5]], [B_enb, PB[5]], scale=-1.0)
        vop(lambda e: e.tensor_mul(KtT[:, :, :], kTh[:, :, tc0:tc0 + 128], enb[:, :, :]), [B_kTh, B_enb], [B_KtT])
        yield
        mm(ps[:, 6, 0:384], uu[:, :], lf_tm[:, j, :], True, True, [B_lf[j], B_const], PB[6])
        act(osq[:, :], ps[:, 6, 0:384], AF.Exp, [PB[6]], [B_osq, PB[6]])
        vop(lambda e: e.tensor_mul(Khat[:, :], k_tm[:, j, :], osq[:, :]), [B_ktm[j], B_osq], [B_Khat])
        yield
        if kind == "p":
            qz4 = sap(Qz, 0, [[QZN, 64], [G * 128, 6], [128 + C, G], [1, C]])
            vop(lambda e: e.tensor_mul(qz4, qTh[:, :, tc0:tc0 + 128].rearrange("p h (g c) -> p h g c", g=G),
                                       eb[:, :, :].rearrange("p h (g c) -> p h g c", g=G)), [B_qTh, B_eb], [B_Qz])
            for g in range(G):
                vop(lambda e, g=g: e.tensor_scalar_mul(Khz[:, g, :], Khat[:, :], rowmP[:, g:g + 1]), [B_Khat, B_c2], [B_Khz])
            for h in range(6):
                bk = 4 if h < 4 else 5
                qfull = sap(Qz, h * G * 128, [[QZN, 64], [128 + C, G], [1, C]])
                mm(ps[:, bk, (h % 4) * 128:(h % 4 + 1) * 128], KtT[:, h, :], qfull, True, True, [B_Qz, B_KtT], PB[bk])
            vop(lambda e: e.tensor_tensor(Am[:, 0:4, :], ps[:, 4, :].rearrange("p (h t) -> p h t", h=4),
                                          sap(trib, 0, [[128, 128], [0, 4], [1, 128]]), ALU.mult), [PB[4], B_c2], [B_Am, PB[4]])
            vop(lambda e: e.tensor_tensor(Am[:, 4:6, :], ps[:, 5, 0:256].rearrange("p (h t) -> p h t", h=2),
                                          sap(trib, 0, [[128, 128], [0, 2], [1, 128]]), ALU.mult), [PB[5], B_c2], [B_Am, PB[5]])
            yield
            first = True
            for h in range(6):
                mm(ps[:, 7, h * 64:(h + 1) * 64], Am[:, h, :], Vh[:, j, h * 64:(h + 1) * 64], first, False, [B_Am, B_Vh[j]], PB[7])
                first = False
            for g in range(G):
                for h in range(6):
                    mm(ps[:, 7, h * 64:(h + 1) * 64], Qz[:, (h * G + g) * 128:(h * G + g + 1) * 128], Sbf[:, l, h, :], False, False,
                       [B_Qz, B_Sbf[l]], PB[7])
                for h in range(6):
                    mm(ps[0:64, 6, h * 64:(h + 1) * 64], Khz[:, g, h * 64:(h + 1) * 64], Vh[:, j, h * 64:(h + 1) * 64], h == 0, True,
                       [B_Khz, B_Vh[j]], PB[6])
                ebl = sap(eb, g * C + C - 1, [[6 * 128, 64], [128, 6], [0, 64]])
                vop(lambda e, ebl=ebl: e.tensor_mul(Sst[:, l, :, :], Sst[:, l, :, :], ebl), [B_S[l], B_eb], [B_S[l]])
                vop(lambda e: e.tensor_add(Sst[:, l, :, :], Sst[:, l, :, :], ps[0:64, 6, 0:384].rearrange("p (h v) -> p h v", h=6)),
                    [B_S[l], PB[6]], [B_S[l], PB[6]])
                act(Sbf[:, l, :, :], Sst[:, l, :, :], AF.Copy, [B_S[l]], [B_Sbf[l]])
                yield
        else:
            for h in range(6):
                P.dma("sp", S0[:, :, :], I["sh"][l][:, h, :, :].rearrange("b k v -> k b v"), writes=[B_S0], sbuf=B_S0)
                act(S0b[:, :, :], S0[:, :, :], AF.Copy, [B_S0], [B_S0b])
                qz3 = sap(Qz, 0, [[QZN, 64], [128 + C, G], [1, C]])
                vop(lambda e, h=h, qz3=qz3: e.tensor_mul(qz3, qTh[:, h, 0:128].rearrange("p (g c) -> p g c", g=G),
                                                         eb[:, h, :].rearrange("p (g c) -> p g c", g=G)), [B_qTh, B_eb], [B_Qz])
                mm(ps[:, 4, 0:128], KtT[:, h, :], qz3, True, True, [B_Qz, B_KtT], PB[4])
                vop(lambda e, h=h: e.tensor_tensor(Am[:, h, :], ps[:, 4, 0:128], trib[:, :], ALU.mult), [PB[4], B_c2], [B_Am, PB[4]])
                mm(ps[:, 7, h * 64:(h + 1) * 64], Am[:, h, :], Vh[:, j, h * 64:(h + 1) * 64], h == 0, False, [B_Am, B_Vh[j]], PB[7])
                for b in range(16):
                    mm(ps[:, 7, h * 64:(h + 1) * 64], Qz[:, b * 128:(b + 1) * 128], S0b[:, b, :], False, False, [B_Qz, B_S0b], PB[7])
                vop(lambda e, h=h: e.tensor_tensor(Vz[:, :, :], sap(Vh, j * 384 + h * 64, [[TPB * 384, 128], [0, 16], [1, 64]]),
                                                   sap(rowmS, 0, [[GS, 128], [1, 16], [0, 64]]), ALU.mult), [B_Vh[j], B_c2], [B_Vz])
                for half in range(2):
                    mm(ps[0:64, 5, 0:512], Khat[:, h * 64:(h + 1) * 64], Vz[:, half * 8:half * 8 + 8, :].rearrange("p b v -> p (b v)"),
                       True, True, [B_Khat, B_Vz], PB[5])
                    ebl = sap(eb, h * 128 + half * 64 + C - 1, [[6 * 128, 64], [C, 8], [0, 64]])
                    vop(lambda e, half=half, ebl=ebl: e.tensor_mul(S0[:, half * 8:half * 8 + 8, :], S0[:, half * 8:half * 8 + 8, :], ebl),
                        [B_S0, B_eb], [B_S0])
                    vop(lambda e, half=half: e.tensor_add(S0[:, half * 8:half * 8 + 8, :], S0[:, half * 8:half * 8 + 8, :],
                                                          ps[0:64, 5, 0:512].rearrange("p (b v) -> p b v", b=8)),
                        [B_S0, PB[5]], [B_S0, PB[5]])
                P.dma("sp", O["hs"][l][:, h, :, :].rearrange("b k v -> k b v"), S0[:, :, :], reads=[B_S0], sbuf=B_S0)
        yield
        act(osq[:, :], ps[:, 7, 0:384], AF.Square, [PB[7]], [B_osq, PB[7]])
        vop(lambda e: e.tensor_reduce(oss[:, 0:6], osq[:, :].rearrange("p (h v) -> p h v", h=6), AX.X, ALU.add), [B_osq], [B_oss])
        act(oss[:, 0:6], oss[:, 0:6], AF.Ln, [B_oss, B_const], [B_oss], scale=1.0 / 64, bias=epsb[:, 0:1])
        act(oss[:, 0:6], oss[:, 0:6], AF.Exp, [B_oss], [B_oss], scale=-0.5)
        vop(lambda e: e.tensor_tensor(osq[:, :].rearrange("p (h v) -> p h v", h=6), ps[:, 7, 0:384].rearrange("p (h v) -> p h v", h=6),
                                      sap(oss, 0, [[8, 128], [1, 6], [0, 64]]), ALU.mult), [PB[7], B_oss], [B_osq, PB[7]])
        vop(lambda e: e.tensor_mul(onb[:, :], osq[:, :], gate[:, j, :]), [B_osq, B_gate[j]], [B_onb])
        for m3 in range(3):
            tr(psbf(3)[:, m3 * 128:(m3 + 1) * 128], onb[:, m3 * 128:(m3 + 1) * 128], identb[:, :], [B_onb, B_c2], PB[3])
        act(hm[:, 3:6, tc0:tc0 + 128], psbf(3)[:, 0:384].rearrange("p (h t) -> p h t", h=3), AF.Copy,
            [PB[3]], [B_hm[3], B_hm[4], B_hm[5], PB[3]])

    def out_proj(l, blkinfo):
        t0, nt, kind, bi = blkinfo
        for c in range(4):
            wt_, wb = wget("wo", l, c)
            wt = wt_[:, 0:2048].rearrange("p (a b) -> p a b", a=8)
            for m2 in range(2):
                m = c * 2 + m2
                bk = 1 + (m % 2)
                for k in range(8):
                    mm(ps[:, bk, 0:nt], wt[:, k, m2 * 128:(m2 + 1) * 128], hm[:, k, 0:nt], k == 0, k == 7, [wb, B_hm[k]], PB[bk])
                resid_update(l, blkinfo, m, bk, 16)

    actT = sb("actT", [128, NF, NTB], BF16)
    B_act = [Buf("act%d" % f) for f in range(NF)]

    def ffn(l, blkinfo):
        t0, nt, kind, bi = blkinfo
        norm_block(l, blkinfo, 1)
        for f in range(NF):
            wt, wb = wget("fi", l, f)
            gb, ub = (4, 5) if f % 2 == 0 else (6, 7)
            for k in range(8):
                mm(ps[:, gb, 0:nt], wt[:, k * 128:(k + 1) * 128], hm[:, k, 0:nt], k == 0, k == 7, [wb, B_hm[k]], PB[gb])
            for k in range(8):
                mm(ps[:, ub, 0:nt], wt[:, 1024 + k * 128:1024 + (k + 1) * 128], hm[:, k, 0:nt], k == 0, k == 7, [wb, B_hm[k]], PB[ub])
            s = f % 2
            act(et[s][:, 0:nt], ps[:, gb, 0:nt], AF.Silu, [PB[gb]], [B_et[s], PB[gb]])
            vop(lambda e, f=f, s=s, ub=ub: e.tensor_tensor(actT[:, f, 0:nt], ps[:, ub, 0:nt], et[s][:, 0:nt], ALU.mult),
                [PB[ub], B_et[s]], [B_act[f], PB[ub]])
        for m in range(8):
            wt_, wb = wget("fo", l, m)
            wt = wt_[:, 0:NF * 128].rearrange("p (a b) -> p a b", a=NF)
            bk = 1 + (m % 2)
            for f in range(NF):
                mm(ps[:, bk, 0:nt], wt[:, f, :], actT[:, f, 0:nt], f == 0, f == NF - 1, [wb, B_act[f]], PB[bk])
            resid_update(l, blkinfo, m, bk, 40)

    def final_block(blkinfo):
        t0, nt, kind, bi = blkinfo
        ssq_rstd(nt)
        ys = 1
        for j in range(nt // 128):
            for half in range(2):
                bk = 4 + half
                for kk in range(4):
                    k = half * 4 + kk
                    s = k % 2
                    vop(lambda e, k=k, s=s, j=j: e.tensor_mul(ntmp[s][:, 0:128], xTb[:, k, j * 128:(j + 1) * 128], rstd[:, j * 128:(j + 1) * 128]),
                        [B_xT[k], B_rstd], [B_ntmp[s]])
                    act(ntmp[s][:, 128:256], ntmp[s][:, 0:128], AF.Copy, [B_ntmp[s], B_c3], [B_ntmp[s]], scale=fnw_fm[:, k:k + 1])
                    tr(ps[:, bk, kk * 128:(kk + 1) * 128], ntmp[s][:, 128:256], ident[:, :], [B_ntmp[s], B_const], PB[bk])
                if half == 0:
                    act(xst[ys][:, 0:512], ps[:, bk, :], AF.Copy, [PB[bk]], [B_xst[ys], PB[bk]])
                else:
                    vop(lambda e, bk=bk: e.tensor_copy(xst[ys][:, 512:1024], ps[:, bk, :]), [PB[bk]], [B_xst[ys], PB[bk]])
            dst = O["yp"][t0 + j * 128:t0 + (j + 1) * 128, :] if kind == "p" else O["ys"]
            P.dma("pool", dst, xst[ys][:, :], reads=[B_xst[ys]], sbuf=B_xst[ys])

    KTs = sb("KTs", [128, 3, 128], BF16)
    Vs = sb("Vs", [128, 384], BF16)
    B_KTs, B_Vs = [Buf("KTs")], [Buf("Vs")]
    Qblk = sb("Qblk", [128, 3, 16, 2, 8], BF16)
    B_Qblk = Buf("Qblk")
    Kcs = [KT[:, 0, :, :].rearrange("p a b -> p (a b)").rearrange("p (r f) -> p r f", r=16),
           Vb[:, 1, :].rearrange("p (r f) -> p r f", r=16)]
    KcT = KT[:, 1, :, :].rearrange("p a b -> p (a b)").rearrange("p (i t) -> p i t", t=128)
    Vc = Vb[:, 0, :].rearrange("p (r f) -> p r f", r=16)
    B_Kcs, B_Vc = [Buf("Kc0"), Buf("Kc1")], Buf("Vc")
    B_KcT = [Buf("KcT%d" % i) for i in range(6)]
    Ps = sb("Ps", [128, 768], BF16)
    B_Ps = Buf("Ps")
    Pn = sb("Pn", [128, 256], BF16)
    B_Pn = Buf("Pn")
    rds = sb("rds", [128, 768])
    B_rds = Buf("rds")

    def attn_sample(l):
        vop(lambda e: e.memset(Qblk[:, :, :, :, :], 0.0), [], [B_Qblk])
        for i in range(2):
            vop(lambda e, i=i: e.memset(Kcs[i][0:96, 8:16, :], 0.0), [], [B_Kcs[i]], eng="pool")
        vop(lambda e: e.memset(Vc[0:96, 8:16, :], 0.0), [], [B_Vc], eng="pool")
        for h2 in range(2):
            vop(lambda e, h2=h2: e.tensor_copy(Qblk[64 * h2:64 * h2 + 64, :, :, h2, :],
                                               QT[64 * h2:64 * h2 + 64, :, 0:128].rearrange("p h (b q) -> p h b q", b=16)),
                [B_QT[0]], [B_Qblk])
        for b in range(16):
            Kc, B_Kc = Kcs[b % 2], B_Kcs[b % 2]
            ckv = I["ck"][l][b].rearrange("(g r) f -> g r f", r=16)
            cvv = I["cv"][l][b].rearrange("(g r) f -> g r f", r=16)
            P.dma("pool", Kc[0:96, 0:8, :], ckv[0:96, 0:8, :], writes=[B_Kc], sbuf=B_Kc)
            P.dma("pool", Kc[96:128, :, :], ckv[96:128, :, :], writes=[B_Kc], sbuf=B_Kc)
            P.dma("pool", Vc[0:96, 0:8, :], cvv[0:96, 0:8, :], writes=[B_Vc], sbuf=B_Vc)
            P.dma("pool", Vc[96:128, :, :], cvv[96:128, :, :], writes=[B_Vc], sbuf=B_Vc)
            for grp in range(6):
                bk = 4 + (grp % 2)
                for i in range(8):
                    idx = grp * 8 + i
                    r, hp = idx // 3, idx % 3
                    tr(psbf(bk)[:, i * 128:(i + 1) * 128], Kc[:, r, hp * 128:(hp + 1) * 128], identb[:, :], [B_Kc, B_c2], PB[bk])
                if grp % 2 == 0:
                    act(KcT[:, grp * 8:grp * 8 + 8, :], psbf(bk)[:, :].rearrange("p (i t) -> p i t", i=8), AF.Copy, [PB[bk]], [B_KcT[grp], PB[bk]])
                else:
                    vop(lambda e, grp=grp, bk=bk: e.tensor_copy(KcT[:, grp * 8:grp * 8 + 8, :], psbf(bk)[:, :].rearrange("p (i t) -> p i t", i=8)),
                        [PB[bk]], [B_KcT[grp], PB[bk]])
            for r in range(16):
                bk = 6 + r // 8
                for hp in range(3):
                    idx = r * 3 + hp
                    c0 = ((r % 8) * 3 + hp) * 16
                    mm(ps[:, bk, c0:c0 + 16], KcT[:, idx, :], Qblk[:, hp, b, :, :].rearrange("p a q -> p (a q)"), True, True,
                       [B_KcT[idx // 8], B_Qblk], PB[bk])
            act(Ps[:, 0:384], ps[:, 6, 0:384], AF.Exp, [PB[6]], [B_Ps, PB[6]], scale=0.125)
            act(Ps[:, 384:768], ps[:, 7, 0:384], AF.Exp, [PB[7]], [B_Ps, PB[7]], scale=0.125)
            vop(lambda e: e.tensor_tensor(Ps[:, :].rearrange("p (r a q) -> p r a q", r=16, a=6),
                                          Ps[:, :].rearrange("p (r a q) -> p r a q", r=16, a=6),
                                          sap(mc, 0, [[128, 128], [8, 16], [0, 6], [1, 8]]), ALU.mult), [B_Ps, B_c2], [B_Ps])
            for hp in range(3):
                ob = 0 if hp < 2 else 1
                oc = (hp % 2) * 256 + b * 16
                for r in range(16):
                    first = (b == 0 and r == 0 and hp in (0, 2))
                    mm(ps[:, ob, oc:oc + 16], Vc[:, r, hp * 128:(hp + 1) * 128], Ps[:, (r * 3 + hp) * 16:(r * 3 + hp) * 16 + 16], first, False,
                       [B_Vc, B_Ps], PB[ob])
            db = 2 if b < 10 else 3
            dc = (b if b < 10 else b - 10) * 48
            for r in range(16):
                first = (r == 0 and b in (0, 10))
                mm(ps[:, db, dc:dc + 48], onesb[:, :], Ps[:, r * 48:(r + 1) * 48], first, False, [B_const, B_Ps], PB[db])
        vop(lambda e: e.tensor_copy(rds[:, 0:480], ps[:, 2, 0:480]), [PB[2]], [B_rds, PB[2]])
        vop(lambda e: e.tensor_copy(rds[:, 480:768], ps[:, 3, 0:288]), [PB[3]], [B_rds, PB[3]])
        for hp in range(3):
            mm(ps[:, 4, 0:256], KTs[:, hp, :], Qblk[:, hp, :, :, :].rearrange("p b a q -> p (b a q)"), True, True, [B_KTs[0], B_Qblk], PB[4])
            act(Pn[:, :], ps[:, 4, 0:256], AF.Exp, [PB[4]], [B_Pn, PB[4]], scale=0.125)
            vop(lambda e: e.tensor_tensor(Pn[:, :].rearrange("p (b a q) -> p b a q", b=16, a=2),
                                          Pn[:, :].rearrange("p (b a q) -> p b a q", b=16, a=2),
                                          sap(mnew, 0, [[128, 128], [8, 16], [0, 2], [1, 8]]), ALU.mult), [B_Pn, B_c2], [B_Pn])
            ob = 0 if hp < 2 else 1
            oc = (hp % 2) * 256
            mm(ps[:, ob, oc:oc + 256], Vs[:, hp * 128:(hp + 1) * 128], Pn[:, :], False, True, [B_Vs[0], B_Pn], PB[ob])
            mm(ps[:, 5, 0:256], onesb[:, :], Pn[:, :], True, True, [B_const, B_Pn], PB[5])
            rv = rds[:, :].rearrange("p (b x a q) -> p b x a q", b=16, x=3, a=2)[:, :, hp, :, :]
            vop(lambda e, rv=rv: e.tensor_tensor(rv, rv, ps[:, 5, 0:256].rearrange("p (b a q) -> p b a q", b=16, a=2), ALU.add),
                [B_rds, PB[5]], [B_rds, PB[5]])
        vop(lambda e: e.reciprocal(rds[:, :], rds[:, :]), [B_rds], [B_rds])
        for hp in range(3):
            ob = 0 if hp < 2 else 1
            oc = (hp % 2) * 256
            for h2 in range(2):
                pb = 64 * h2
                ov = ps[pb:pb + 64, ob, oc:oc + 256].rearrange("p (b a q) -> p b a q", b=16, a=2)[:, :, h2, :]
                rv = rds[pb:pb + 64, :].rearrange("p (b x a q) -> p b x a q", b=16, x=3, a=2)[:, :, hp, h2, :]
                vop(lambda e, ov=ov, rv=rv, hp=hp, pb=pb: e.tensor_tensor(hm[pb:pb + 64, hp, 0:128].rearrange("p (b q) -> p b q", b=16), ov, rv, ALU.mult),
                    [PB[ob], B_rds], [B_hm[hp], PB[ob]])

    nblocks = len(BLOCKS) if stage >= 99 else min(len(BLOCKS), stage)
    for blkinfo in BLOCKS[:nblocks] if stage >= 99 or stage < 50 else BLOCKS:
        t0, nt, kind, bi = blkinfo
        if kind == "s":
            P.barrier()
            vop(lambda e: e.memset(Qz[:, :], 0.0), [], [B_Qz])
        load_x(blkinfo)
        for l in range(DEPTH):
            norm_block(l, blkinfo, 0)
            if kind == "p":
                tokmajor_group(l, blkinfo, C_QA, "qa", cons_q(l, blkinfo))
                tokmajor_group(l, blkinfo, C_KA, "ka",
                               cons_k(l, blkinfo, lambda tj, l=l: KT[:, l, :, tj * 128:(tj + 1) * 128], B_KT[l], O["kp"][l]))
                tokmajor_group(l, blkinfo, C_VA, "va",
                               cons_v(l, blkinfo, lambda tj, l=l: Vb[:, l, tj * 384:(tj + 1) * 384], B_V[l], O["vp"][l]))
            else:
                tokmajor_group(l, blkinfo, C_QA, "qa", cons_q(l, blkinfo))
                tokmajor_group(l, blkinfo, C_KA, "ka", cons_k(l, blkinfo, lambda tj: KTs[:, :, :], B_KTs, O["ks"][l]))
                tokmajor_group(l, blkinfo, C_VA, "va", cons_v(l, blkinfo, lambda tj: Vs[:, :], B_Vs, O["vs"][l]))
            tokmajor_group(l, blkinfo, C_FR, "fr", cons_fr(l, blkinfo))
            tokmajor_group(l, blkinfo, C_IR, "ir", cons_ir(l, blkinfo))
            tokmajor_group(l, blkinfo, C_GR, "gr", cons_gr(l, blkinfo))
            conv_pre(l, blkinfo)
            featmajor_groups(l, blkinfo)
            if kind == "p":
                conv_block(l, blkinfo)
                ag = attn_prompt(l, bi)
                n_att = 6 * (TPB * bi + TPB + 1)
                n_hg = (nt // 128) * 9

                def hg_chain(l=l, blkinfo=blkinfo, nt=nt):
                    for j in range(nt // 128):
                        yield from hgrn_tile(l, blkinfo, j)
                hg = hg_chain()
                per = max(1, -(-n_att // n_hg))
                a_done = h_done = False
                while not (a_done and h_done):
                    if not h_done:
                        try:
                            next(hg)
                        except StopIteration:
                            h_done = True
                    for _ in range(per):
                        if a_done:
                            break
                        try:
                            next(ag)
                        except StopIteration:
                            a_done = True
            else:
                attn_sample(l)
                conv_block(l, blkinfo)
                for j in range(nt // 128):
                    for _ in hgrn_tile(l, blkinfo, j):
                        pass
            if kind == "p" and bi == NPB - 1:
                P.dma("sp", O["hp"][l].rearrange("h k v -> k h v"), Sst[:, l, :, :], reads=[B_S[l]], sbuf=B_S[l])
            out_proj(l, blkinfo)
            ffn(l, blkinfo)
        final_block(blkinfo)

    P.wait_dmas("sp")
    P.emit()
    es.close()
    P.close()
    return nc, P.stats


DEFAULT_STAGE = 99
_CACHE = {}


def kernel(**inputs):
    x_prompt = np.asarray(inputs["x_prompt"], np.float32)
    consts = host_consts()
    stage = int(inputs.pop("_stage")) if "_stage" in inputs else DEFAULT_STAGE
    if "prog" not in _CACHE:
        _CACHE["prog"] = build_program(stage)
    nc, stats = _CACHE["prog"]
    in_maps = []
    g = lambda n: np.asarray(inputs[n], np.float32)
    for c in range(8):
        m = {}
        m["xp"] = np.ascontiguousarray(g("x_prompt")[c])
        m["xs"] = np.ascontiguousarray(g("x_sample")[16 * c:16 * c + 16].reshape(128, D))
        m["call"] = np.ascontiguousarray(np.concatenate([g("c_prompt")[c:c + 1], g("c_sample")[16 * c:16 * c + 16]], axis=0))
        m["ck"] = np.ascontiguousarray(g("cache_k")[:, 16 * c:16 * c + 16].reshape(DEPTH, 16, 2048, 384))
        m["cv"] = np.ascontiguousarray(g("cache_v")[:, 16 * c:16 * c + 16].reshape(DEPTH, 16, 2048, 384))
        m["sh"] = np.ascontiguousarray(g("state_hgrn")[:, 16 * c:16 * c + 16])
        m["sc"] = np.ascontiguousarray(g("state_conv")[:, 16 * c:16 * c + 16])
        m["w_ada"] = g("w_ada")
        m["b_ada"] = g("b_ada").reshape(DEPTH, 48, 128)
        m["w_in"] = g("w_in")
        m["conv_w"] = g("conv_w")
        m["conv_b"] = g("conv_b")
        m["lbl"] = g("hgrn_lb_logits")
        m["nw"] = g("hgrn_norm_w")
        m["w_out"] = g("w_out")
        m["w_ffn_in"] = g("w_ffn_in")
        m["w_ffn_out"] = g("w_ffn_out")
        m["fnw"] = g("final_norm_w").reshape(8, 128)
        for n, v in consts.items():
            m["c_" + n] = v
        in_maps.append(m)
    res = run_bass_kernel_spmd(nc, in_maps, core_ids=list(range(8)))
    R = res.results
    cat = lambda n: np.stack([R[c][n] for c in range(8)], axis=0)
    y_p = cat("yp")
    y_s = np.concatenate([R[c]["ys"].reshape(16, 8, D) for c in range(8)], axis=0)
    k_p = np.stack([R[c]["kp"] for c in range(8)], axis=1).reshape(DEPTH, 8, SEQ, 6, 64)
    v_p = np.stack([R[c]["vp"] for c in range(8)], axis=1).reshape(DEPTH, 8, SEQ, 6, 64)
    h_p = np.stack([R[c]["hp"] for c in range(8)], axis=1)
    cv_p = np.stack([R[c]["cvp"] for c in range(8)], axis=1)
    k_s = np.concatenate([R[c]["ks"].reshape(DEPTH, 16, 8, 6, 64) for c in range(8)], axis=1)
    v_s = np.concatenate([R[c]["vs"].reshape(DEPTH, 16, 8, 6, 64) for c in range(8)], axis=1)
    h_s = np.concatenate([R[c]["hs"] for c in range(8)], axis=1)
    cv_s = np.concatenate([R[c]["cvs"] for c in range(8)], axis=1)
    return (y_p, y_s, k_p, v_p, h_p, cv_p, k_s, v_s, h_s, cv_s)
```
